# Optimizing a Trainium2 kernel written in Bass

```python
import jax, jax.numpy as jnp
from jax import lax
import numpy as np

D_MODEL = 1024
BATCH = 8
SEQ = 4096
DEPTH = 1

EPS = 1e-6
HG_HEADS = 8
HG_DK = 128
HG_DV = 128
HG_KW = HG_HEADS * HG_DK
HG_WIDTH = HG_HEADS * HG_DV
HG_CHUNK = 64
M2_HEADDIM = 64
M2_WIDTH = D_MODEL
M2_HEADS = M2_WIDTH // M2_HEADDIM
M2_GROUPS = 2
M2_STATE = 128
M2_CONV = 4
M2_CHUNK = 128
M2_CONV_DIM = M2_WIDTH + 2 * M2_GROUPS * M2_STATE
MIX_WIDTH = HG_WIDTH + M2_WIDTH
IN_COLS = 2 * HG_KW + 2 * HG_WIDTH + M2_WIDTH + M2_CONV_DIM + M2_HEADS

kernel_name = "hymba_style_hgrn2_mamba2_hybrid"


def rmsnorm(x, w):
    xf = x.astype(jnp.float32)
    y = xf * lax.rsqrt(jnp.mean(xf * xf, axis=-1, keepdims=True) + EPS)
    return (y * w.astype(jnp.float32)).astype(x.dtype)


def hgrn2_mix(q_raw, f_raw, i_raw, g_raw, lb, norm_w):
    B_, L, _ = q_raw.shape
    C = HG_CHUNK
    n = L // C
    lbf = lb.astype(jnp.float32)
    fl = f_raw.astype(jnp.float32)
    q = jax.nn.silu(q_raw.astype(jnp.float32))
    f = lbf + (1.0 - lbf) * jax.nn.sigmoid(fl)
    k = (1.0 - lbf) * jax.nn.sigmoid(-fl)
    logf = jnp.log(f)
    v = i_raw.astype(jnp.float32)

    def to_chunks(t, d):
        return t.reshape(B_, n, C, HG_HEADS, d).transpose(1, 0, 3, 2, 4)

    mask = jnp.tril(jnp.ones((C, C), dtype=bool))

    def step(S, inp):
        qc, kc, vc, gc = inp
        b = jnp.cumsum(gc, axis=2)
        o_inter = jnp.einsum('bhtk,bhkv->bhtv', qc * jnp.exp(b), S)
        diff = b[:, :, :, None, :] - b[:, :, None, :, :]
        decay = jnp.exp(jnp.where(mask[:, :, None], diff, -jnp.inf))
        att = jnp.einsum('bhtk,bhsk,bhtsk->bhts', qc, kc, decay)
        o_intra = jnp.einsum('bhts,bhsv->bhtv', att, vc)
        b_last = b[:, :, -1:, :]
        S_new = jnp.exp(b_last[:, :, 0, :])[..., None] * S + jnp.einsum(
            'bhsk,bhsv->bhkv', kc * jnp.exp(b_last - b), vc)
        return S_new, o_inter + o_intra

    S0 = jnp.zeros((B_, HG_HEADS, HG_DK, HG_DV), jnp.float32)
    _, o = lax.scan(step, S0, (to_chunks(q, HG_DK), to_chunks(k, HG_DK),
                               to_chunks(v, HG_DV), to_chunks(logf, HG_DK)))
    o = o.transpose(1, 0, 3, 2, 4).reshape(B_, L, HG_HEADS, HG_DV)
    o = o * lax.rsqrt(jnp.mean(o * o, axis=-1, keepdims=True) + EPS)
    o = o * norm_w.astype(jnp.float32).reshape(HG_HEADS, HG_DV)
    o = o.reshape(B_, L, HG_WIDTH) * jax.nn.silu(g_raw.astype(jnp.float32))
    return o.astype(q_raw.dtype)


def causal_dwconv(u, w, b):
    ch = u.shape[-1]
    out = lax.conv_general_dilated(
        u, w[:, None, :].astype(u.dtype), window_strides=(1,),
        padding=[(M2_CONV - 1, 0)], dimension_numbers=('NWC', 'WIO', 'NWC'),
        feature_group_count=ch)
    return out + b.astype(u.dtype)


def ssd_scan(x, dt, A, Bm, Cm):
    B_, L, H, P = x.shape
    C = M2_CHUNK
    n = L // C
    G = M2_GROUPS
    hg = H // G
    N = M2_STATE
    xc = x.reshape(B_, n, C, G, hg, P).transpose(1, 0, 3, 4, 2, 5)
    dtc = dt.reshape(B_, n, C, G, hg).transpose(1, 0, 3, 4, 2)
    ac = dtc * A.reshape(G, hg)[None, None, :, :, None]
    Bc = Bm.reshape(B_, n, C, G, N).transpose(1, 0, 3, 2, 4)
    Cc = Cm.reshape(B_, n, C, G, N).transpose(1, 0, 3, 2, 4)
    mask = jnp.tril(jnp.ones((C, C), dtype=bool))

    def step(state, inp):
        xk, dtk, ak, bk, ck = inp
        cs = jnp.cumsum(ak, axis=-1)
        seg = cs[..., :, None] - cs[..., None, :]
        Lmat = jnp.exp(jnp.where(mask, seg, -jnp.inf))
        cb = jnp.einsum('bgtn,bgsn->bgts', ck, bk)
        w = cb[:, :, None] * Lmat * dtk[..., None, :]
        y_diag = jnp.einsum('bghts,bghsp->bghtp', w, xk)
        y_off = jnp.einsum('bgtn,bghpn->bghtp', ck, state) * jnp.exp(cs)[..., None]
        decay_s = jnp.exp(cs[..., -1:] - cs) * dtk
        state_new = jnp.exp(cs[..., -1])[..., None, None] * state + jnp.einsum(
            'bghs,bghsp,bgsn->bghpn', decay_s, xk, bk)
        return state_new, y_diag + y_off

    s0 = jnp.zeros((B_, G, hg, P, N), jnp.float32)
    _, y = lax.scan(step, s0, (xc, dtc, ac, Bc, Cc))
    return y.transpose(1, 0, 4, 2, 3, 5).reshape(B_, L, H, P)


def mamba2_mix(z, xbc, dt_raw, conv_w, conv_b, dt_bias, a_log, d_skip, norm_w):
    B_, L, _ = z.shape
    xbc = jax.nn.silu(causal_dwconv(xbc, conv_w, conv_b)).astype(jnp.float32)
    xs = xbc[..., :M2_WIDTH].reshape(B_, L, M2_HEADS, M2_HEADDIM)
    Bm = xbc[..., M2_WIDTH:M2_WIDTH + M2_GROUPS * M2_STATE].reshape(B_, L, M2_GROUPS, M2_STATE)
    Cm = xbc[..., M2_WIDTH + M2_GROUPS * M2_STATE:].reshape(B_, L, M2_GROUPS, M2_STATE)
    dt = jax.nn.softplus(dt_raw.astype(jnp.float32) + dt_bias.astype(jnp.float32))
    A = -jnp.exp(a_log.astype(jnp.float32))
    y = ssd_scan(xs, dt, A, Bm, Cm) + d_skip.astype(jnp.float32)[:, None] * xs
    y = y.reshape(B_, L, M2_WIDTH) * jax.nn.silu(z.astype(jnp.float32))
    y = y.reshape(B_, L, M2_GROUPS, M2_WIDTH // M2_GROUPS)
    y = y * lax.rsqrt(jnp.mean(y * y, axis=-1, keepdims=True) + EPS)
    y = y.reshape(B_, L, M2_WIDTH) * norm_w.astype(jnp.float32)
    return y.astype(z.dtype)


def setup_inputs(seed: int = 0) -> dict:
    key = jax.random.key(seed)
    ks = jax.random.split(key, 16)
    f32 = jnp.float32
    x = jax.random.normal(ks[0], (BATCH, SEQ, D_MODEL), f32)
    norm_w = 1.0 + 0.02 * jax.random.normal(ks[1], (DEPTH, D_MODEL), f32)
    w_in = jax.random.normal(ks[2], (DEPTH, D_MODEL, IN_COLS), f32) * D_MODEL ** -0.5
    hg_lb_logits = 0.1 * jax.random.normal(ks[3], (DEPTH + 1, HG_KW), f32)
    hg_norm_w = 1.0 + 0.02 * jax.random.normal(ks[4], (DEPTH, HG_WIDTH), f32)
    m2_conv_w = jax.random.uniform(ks[5], (DEPTH, M2_CONV, M2_CONV_DIM), f32, -1.0, 1.0) * M2_CONV ** -0.5
    m2_conv_b = 0.02 * jax.random.normal(ks[6], (DEPTH, M2_CONV_DIM), f32)
    dt0 = jnp.exp(jax.random.uniform(ks[7], (DEPTH, M2_HEADS), f32, np.log(1e-3), np.log(1e-1)))
    m2_dt_bias = dt0 + jnp.log(-jnp.expm1(-dt0))
    m2_a_log = jnp.log(jax.random.uniform(ks[8], (DEPTH, M2_HEADS), f32, 1.0, 16.0))
    m2_d_skip = 1.0 + 0.02 * jax.random.normal(ks[9], (DEPTH, M2_HEADS), f32)
    m2_norm_w = 1.0 + 0.02 * jax.random.normal(ks[10], (DEPTH, M2_WIDTH), f32)
    w_out = jax.random.normal(ks[11], (DEPTH, MIX_WIDTH, D_MODEL), f32) * MIX_WIDTH ** -0.5
    final_norm_w = 1.0 + 0.02 * jax.random.normal(ks[12], (D_MODEL,), f32)
    return {"x": x, "norm_w": norm_w, "w_in": w_in, "hg_lb_logits": hg_lb_logits,
            "hg_norm_w": hg_norm_w, "m2_conv_w": m2_conv_w, "m2_conv_b": m2_conv_b,
            "m2_dt_bias": m2_dt_bias, "m2_a_log": m2_a_log, "m2_d_skip": m2_d_skip,
            "m2_norm_w": m2_norm_w, "w_out": w_out, "final_norm_w": final_norm_w}


def reference(x, norm_w, w_in, hg_lb_logits, hg_norm_w, m2_conv_w, m2_conv_b,
              m2_dt_bias, m2_a_log, m2_d_skip, m2_norm_w, w_out, final_norm_w):
    lb_all = jnp.cumsum(jax.nn.softmax(hg_lb_logits.astype(jnp.float32), axis=0), axis=0)
    sizes = [HG_KW, HG_KW, HG_WIDTH, HG_WIDTH, M2_WIDTH, M2_CONV_DIM, M2_HEADS]
    split_idx = [int(s) for s in np.cumsum(sizes)[:-1]]
    for l in range(DEPTH):
        h = rmsnorm(x, norm_w[l])
        proj = jnp.einsum('bsd,de->bse', h, w_in[l])
        q_raw, f_raw, i_raw, g_raw, z, xbc, dt_raw = jnp.split(proj, split_idx, axis=-1)
        hg_out = hgrn2_mix(q_raw, f_raw, i_raw, g_raw, lb_all[l], hg_norm_w[l])
        m2_out = mamba2_mix(z, xbc, dt_raw, m2_conv_w[l], m2_conv_b[l], m2_dt_bias[l],
                            m2_a_log[l], m2_d_skip[l], m2_norm_w[l])
        mix = jnp.concatenate([hg_out, m2_out], axis=-1)
        x = x + jnp.einsum('bse,ed->bsd', mix, w_out[l]).astype(x.dtype)
    return rmsnorm(x, final_norm_w)
```

```python
import numpy as np
import concourse.bass as bass
import concourse.mybir as mybir
from concourse.bass_utils import run_bass_kernel_spmd

F32 = mybir.dt.float32
BF16 = mybir.dt.bfloat16
AF = mybir.ActivationFunctionType
ALU = mybir.AluOpType
AX = mybir.AxisListType

D = 1024
EPS = 1e-6
NPP = 100
NPB = 2608
IN_COLS = 6672


class Buf:
    __slots__ = ("name", "w", "r")

    def __init__(self, name):
        self.name = name
        self.w = None
        self.r = {}


class _Rec:
    def __init__(self):
        self.call = None

    def __getattr__(self, name):
        def f(*a, **k):
            self.call = (name, a, k)
            return self
        return f


def _record(fn):
    r = _Rec()
    fn(r)
    assert r.call is not None
    return r.call


class Prog:
    ENGS = ("pe", "act", "dve", "pool", "sp")

    def __init__(self):
        self.ops = {e: [] for e in self.ENGS}
        self.cnt = {e: 0 for e in self.ENGS}
        self.known = {e: {} for e in self.ENGS}
        self.dmacnt = {}
        self.bufs = {}
        self.pending = {e: [] for e in self.ENGS}
        self.cursor = {e: 0 for e in self.ENGS}

    def barrier(self):
        for e in self.ENGS:
            kn = self.known[e]
            for k in list(self.ENGS[:4]) + list(self.dmacnt.keys()):
                v = self.cnt[k] if k in self.cnt else self.dmacnt[k]
                if k == e or v == 0 or kn.get(k, 0) >= v:
                    continue
                kn[k] = v
                self.pending[e].append((k, v))

    def buf(self, *key):
        b = self.bufs.get(key)
        if b is None:
            b = self.bufs[key] = Buf(key)
        return b

    def _waits(self, eng, reads, writes):
        deps = {}

        def add(k, v):
            if deps.get(k, 0) < v:
                deps[k] = v

        for b in reads:
            if b.w is not None:
                add(*b.w)
        for b in writes:
            if b.w is not None:
                add(*b.w)
            for k, v in b.r.items():
                add(k, v)
        waits = []
        kn = self.known[eng]
        for k, v in deps.items():
            if k == "pe" and eng == "pe":
                continue
            if kn.get(k, 0) >= v:
                continue
            kn[k] = v
            waits.append((k, v))
        if self.pending[eng]:
            waits = self.pending[eng] + waits
            self.pending[eng] = []
        return waits

    def emit(self, eng, fn, reads=(), writes=()):
        waits = self._waits(eng, reads, writes)
        self.cnt[eng] += 1
        idx = self.cnt[eng]
        self.ops[eng].append((waits, _record(fn), (eng, 1), [b.name for b in reads], [b.name for b in writes]))
        for b in writes:
            b.w = (eng, idx)
            b.r = {}
        for b in reads:
            if b.r.get(eng, 0) < idx:
                b.r[eng] = idx
        return idx

    def dma(self, fn, semkey, reads=(), writes=()):
        waits = self._waits("sp", reads, writes)
        self.dmacnt[semkey] = self.dmacnt.get(semkey, 0) + 16
        v = self.dmacnt[semkey]
        self.ops["sp"].append((waits, _record(fn), (semkey, 16), [b.name for b in reads], [b.name for b in writes]))
        for b in writes:
            b.w = (semkey, v)
            b.r = {}
        for b in reads:
            b.r[semkey] = v


def build(L, dbg=()):
    NCH = L // 128
    nc = bass.Bass("TRN2", target_bir_lowering=False)
    x_d = nc.dram_tensor("x", [L, D], F32, kind="ExternalInput").ap()
    win_d = nc.dram_tensor("w_in", [D, IN_COLS], F32, kind="ExternalInput").ap()
    wout_d = nc.dram_tensor("w_out", [2 * D, D], F32, kind="ExternalInput").ap()
    pp_d = nc.dram_tensor("pp", [128, NPP], F32, kind="ExternalInput").ap()
    pb_d = nc.dram_tensor("pb", [1, NPB], F32, kind="ExternalInput").ap()
    y_d = nc.dram_tensor("y", [L, D], F32, kind="ExternalOutput").ap()
    dbg_d = {}

    P = Prog()
    B = P.buf
    E = P.emit
    tiles = {}
    stack = []

    def sb(name, shape, dt):
        g = nc.sbuf_tensor(name, list(shape), dt)
        t = g.__enter__()
        stack.append(g)
        tiles[name] = t
        return t

    pstack = []

    def pst(name, shape, dt):
        g = nc.psum_tensor(name, list(shape), dt)
        t = g.__enter__()
        pstack.append(g)
        return t

    MIXT = sb("MIXT", [128, 8, L], BF16)
    XIN = [sb(f"XIN{i}", [128, 1024], F32) for i in range(2)]
    F32T = [sb(f"F32T{i}", [128, 1024], F32) for i in range(2)]
    HBF = sb("HBF", [128, 1024], BF16)
    PPT = sb("PPT", [128, NPP], F32)
    PBT = sb("PBT", [128, 1072], F32)
    identF = sb("identF", [128, 128], F32)
    identB = sb("identB", [128, 128], BF16)
    maskB = sb("maskB", [128, 128], BF16)
    triF = sb("triF", [128, 128], F32)
    onesF = sb("onesF", [128, 128], F32)
    SMALL = sb("SMALL", [128, 256], F32)
    n_common = len(stack)
    WIN = sb("WIN", [128, 8, 4096], BF16)
    HTA = sb("HTA", [128, 8, 128], BF16)
    QT2 = [sb(f"QT{i}", [128, 8, 128], BF16) for i in range(2)]
    KT2 = [sb(f"KT{i}", [128, 8, 128], BF16) for i in range(2)]
    VC2 = [sb(f"VC{i}", [128, 1024], BF16) for i in range(2)]
    SG2 = [sb(f"SG{i}", [128, 1024], F32) for i in range(2)]
    SC2 = [sb(f"SC{i}", [128, 6, 8], F32) for i in range(2)]
    KTT = sb("KTT", [128, 8, 128], BF16)
    ATT = sb("ATT", [128, 8, 128], BF16)
    SST = sb("SST", [128, 8, 128], F32)
    SBF = sb("SBF", [128, 8, 128], BF16)
    TTA = [sb(f"TTA{i}", [128, 512], F32) for i in range(5)]
    TTB = [sb(f"TTB{i}", [128, 512], F32) for i in range(3)]
    RES1 = sb("RES1", [128, 512], BF16)
    TKV = [sb(f"TKV{i}", [128, 128], F32) for i in range(2)]
    LBT = sb("LBT", [128, 32], F32)
    MIXBA = sb("MIXBA", [128, 1024], BF16)
    print("sbuf bytes remaining after pass A alloc:", nc.sbuf_bytes_remaining)
    PS = [pst(f"PS{i}", [128, 512], F32) for i in range(7)]
    PSB_ = pst("PSB", [128, 1024], BF16)
    PSB2 = [PSB_, PSB_]
    psrr = [0, 0]

    def pbank(stage):
        if stage == 1:
            i = psrr[0] % 4
            psrr[0] += 1
        else:
            i = 4 + psrr[1] % 3
            psrr[1] += 1
        return PS[i], B("ps", i)

    bPSB2 = [B("psb", 0), B("psb", 0)]
    psrrA = [0, 0]

    def pbankA(long_lived):
        if long_lived:
            i = 2 + psrrA[1] % 2
            psrrA[1] += 1
        else:
            i = psrrA[0] % 2
            psrrA[0] += 1
        return PS[i], B("ps", i)

    def T(name):
        return B("t", name)

    sm_off = [0]

    def small(n):
        o = sm_off[0]
        sm_off[0] += n
        assert sm_off[0] <= 256
        return SMALL[:, o:o + n], B("small", o)

    def zipper(U, V, u_lo=0.0, u_hi=1.0):
        items = [(u_lo + (u_hi - u_lo) * (i + 0.5) / len(U), 0, i, u) for i, u in enumerate(U)] + [((i + 0.5) / len(V), 1, i, v) for i, v in enumerate(V)]
        items.sort(key=lambda x: (x[0], x[1], x[2]))
        for _, _, _, f in items:
            f()

    def dump(name, ap, shape, buf, dt=F32):
        if name not in dbg:
            return
        d = nc.dram_tensor("dbg_" + name, list(shape), dt, kind="ExternalOutput").ap()
        dbg_d[name] = d
        P.dma(lambda e, d=d, ap=ap: e.dma_start(out=d, in_=ap), "dbg_" + name, reads=[buf])

    semstack = []
    sems = {}
    for k in ["pe", "act", "dve", "pool", "c0", "c1", "stg0", "stg1", "stg2", "xin0", "xin1", "yout0", "yout1"] + ["dbg_" + n for n in dbg]:
        g = nc.semaphore("s_" + k)
        sems[k] = g.__enter__()
        semstack.append(g)

    def replay_all(final=False):
        def mk(eng_name):
            def run(e):
                ops = P.ops[eng_name]
                for waits, fn, inc, _r, _w in ops[P.cursor[eng_name]:]:
                    for k, v in waits:
                        e.wait_ge(sems[k], v)
                    ins = getattr(e, fn[0])(*fn[1], **fn[2])
                    if inc is not None and inc[0] != "sp":
                        ins.then_inc(sems[inc[0]], inc[1])
                P.cursor[eng_name] = len(ops)
                if final and eng_name == "sp":
                    for k, v in P.dmacnt.items():
                        if k.startswith("yout") or k.startswith("dbg_"):
                            e.wait_ge(sems[k], v)
            return run
        with nc.Block() as block:
            block.sync(mk("sp"))
            block.tensor(mk("pe"))
            block.scalar(mk("act"))
            block.vector(mk("dve"))
            block.gpsimd(mk("pool"))

    P.dma(lambda e: e.dma_start(out=PPT[:], in_=pp_d), "c0", writes=[T("PPT")])
    P.dma(lambda e: e.dma_start(out=PBT[:], in_=pb_d[0:1, 0:1072].partition_broadcast(128)), "c1", writes=[T("PBT")])
    E("pool", lambda e: e.memset(identF[:], 0.0), writes=[T("identF")])
    E("pool", lambda e: e.affine_select(out=identF[:], in_=identF[:], pattern=[[-1, 128]], compare_op=ALU.not_equal,
                                        fill=1.0, base=0, channel_multiplier=1), reads=[T("identF")], writes=[T("identF")])
    E("pool", lambda e: e.tensor_copy(out=identB[:], in_=identF[:]), reads=[T("identF")], writes=[T("identB")])
    E("pool", lambda e: e.memset(onesF[:], 1.0), writes=[T("onesF")])
    E("pool", lambda e: e.affine_select(out=triF[:], in_=onesF[:], pattern=[[1, 128]], compare_op=ALU.is_ge,
                                        fill=0.0, base=0, channel_multiplier=-1), reads=[T("onesF")], writes=[T("triF")])
    E("pool", lambda e: e.tensor_copy(out=maskB[:], in_=triF[:]), reads=[T("triF")], writes=[T("maskB")])
    E("pool", lambda e: e.memset(SST[:], 0.0), writes=[T("SST")])
    E("pool", lambda e: e.memset(RES1[:], 1.0), writes=[T("RES1")])
    E("pool", lambda e: e.memset(RES1[:, :].rearrange("p (h t) -> p h t", h=4)[:, :, 0:1], 0.0), reads=[T("RES1")], writes=[T("RES1")])
    E("dve", lambda e: e.tensor_sub(out=LBT[:, 24:32], in0=PPT[:, 24:32], in1=PPT[:, 32:40]), reads=[T("PPT")], writes=[T("LBT")])
    E("act", lambda e: e.activation(out=LBT[:, 0:8], in_=LBT[:, 24:32], func=AF.Exp, scale=-1.0), reads=[T("LBT")], writes=[T("LBT")])
    E("act", lambda e: e.activation(out=LBT[:, 0:8], in_=LBT[:, 0:8], func=AF.Ln, bias=1.0), reads=[T("LBT")], writes=[T("LBT")])
    E("act", lambda e: e.activation(out=LBT[:, 0:8], in_=LBT[:, 0:8], func=AF.Exp, scale=-1.0), reads=[T("LBT")], writes=[T("LBT")])
    E("dve", lambda e: e.tensor_scalar(out=LBT[:, 8:16], in0=LBT[:, 0:8], scalar1=-1.0, scalar2=1.0, op0=ALU.mult, op1=ALU.add),
      reads=[T("LBT")], writes=[T("LBT")])
    E("act", lambda e: e.activation(out=LBT[:, 16:24], in_=LBT[:, 8:16], func=AF.Ln), reads=[T("LBT")], writes=[T("LBT")])

    stg_rr = [0]

    def cast_block(k_eng, dst_ap, st, ncols, scale_ap, sbuf_, dst_buf):
        if k_eng == "act":
            E("act", lambda e: e.activation(out=dst_ap, in_=st[:, 0:ncols], func=AF.Copy, scale=scale_ap),
              reads=[sbuf_, T("PPT")], writes=[dst_buf])
        else:
            E(k_eng, lambda e: e.tensor_scalar(out=dst_ap, in0=st[:, 0:ncols], scalar1=scale_ap, scalar2=1.0, op0=ALU.mult, op1=ALU.mult),
              reads=[sbuf_, T("PPT")], writes=[dst_buf])

    cast_rr = [0]

    def load_w_in(col0, ncols_total, tag):
        wb = T("WIN")
        for k in range(8):
            c = 0
            while c < ncols_total:
                n = min(1024, ncols_total - c)
                i = stg_rr[0] % 2
                stg_rr[0] += 1
                st = F32T[i]
                sbuf_ = T(f"F32T{i}")
                P.dma(lambda e, st=st, k=k, c=c, n=n: e.dma_start(out=st[:, 0:n], in_=win_d[k * 128:(k + 1) * 128, col0 + c:col0 + c + n]),
                      f"stg{i}", writes=[sbuf_])
                eng = ("act", "dve")[cast_rr[0] % 2]
                cast_rr[0] += 1
                cast_block(eng, WIN[:, k, c:c + n], st, n, PPT[:, k:k + 1], sbuf_, wb)
                c += n

    def load_w_out():
        wb = T("WOUT")
        for k in range(16):
            i = stg_rr[0] % 2
            stg_rr[0] += 1
            st = F32T[i]
            sbuf_ = T(f"F32T{i}")
            P.dma(lambda e, st=st, k=k: e.dma_start(out=st[:, :], in_=wout_d[k * 128:(k + 1) * 128, :]), f"stg{i}", writes=[sbuf_])
            eng = ("act", "dve")[cast_rr[0] % 2]
            cast_rr[0] += 1
            cast_block(eng, WOUT[:, k, :], st, 1024, PPT[:, 8 + k:9 + k], sbuf_, wb)

    def rstd_from_ssq(ssq_ap, ssq_buf, n, out_ap, out_buf):
        E("dve", lambda e: e.tensor_scalar(out=out_ap, in0=ssq_ap, scalar1=1.0 / n, scalar2=EPS, op0=ALU.mult, op1=ALU.add),
          reads=[ssq_buf], writes=[out_buf])
        E("act", lambda e: e.activation(out=out_ap, in_=out_ap, func=AF.Ln), reads=[out_buf], writes=[out_buf])
        E("act", lambda e: e.activation(out=out_ap, in_=out_ap, func=AF.Exp, scale=-0.5), reads=[out_buf], writes=[out_buf])

    ssq1, b_ssq1 = small(1)
    rs1, b_rs1 = small(1)

    def x_load(c):
        sl = c % 2
        P.dma(lambda e: e.dma_start(out=XIN[sl][:], in_=x_d[c * 128:(c + 1) * 128, :]), f"xin{sl}", writes=[T(f"XIN{sl}")])

    def x_norm_T(c, HT, bHT):
        sl = c % 2
        xb = T(f"XIN{sl}")
        E("act", lambda e: e.activation(out=HBF[:], in_=XIN[sl][:], func=AF.Square, accum_out=ssq1), reads=[xb], writes=[T("HBF"), b_ssq1])
        rstd_from_ssq(ssq1, b_ssq1, 1024.0, rs1, b_rs1)
        E("act", lambda e: e.activation(out=HBF[:], in_=XIN[sl][:], func=AF.Copy, scale=rs1), reads=[xb, b_rs1], writes=[T("HBF")])
        for k in range(8):
            E("pe", lambda e, k=k: e.transpose(out=PSB2[0][:, k * 128:(k + 1) * 128], in_=HBF[:, k * 128:(k + 1) * 128], identity=identB[:]),
              reads=[T("HBF"), T("identB")], writes=[bPSB2[0]])
        E("dve", lambda e: e.tensor_copy(out=HT[:, :, :], in_=PSB2[0][:, :].rearrange("p (k t) -> p k t", k=8)), reads=[bPSB2[0]], writes=[bHT])

    def silu_from_psum(dst, dbuf, src, sbuf_, tmp=None, tbuf=None):
        t_, tb_ = (dst, dbuf) if tmp is None else (tmp, tbuf)
        E("act", lambda e: e.activation(out=t_, in_=src, func=AF.Exp, scale=-1.0), reads=[sbuf_], writes=[tb_])
        E("act", lambda e: e.activation(out=t_, in_=t_, func=AF.Ln, bias=1.0), reads=[tb_], writes=[tb_])
        E("act", lambda e: e.activation(out=t_, in_=t_, func=AF.Exp, scale=-1.0), reads=[tb_], writes=[tb_])
        E("dve", lambda e: e.tensor_tensor(out=dst, in0=src, in1=t_, op=ALU.mult), reads=[sbuf_, tb_], writes=[dbuf])

    def proj_tm(HT, bHT, col0, ncols, ps, psbuf, pcol0=0):
        for k in range(8):
            E("pe", lambda e, k=k: e.matmul(ps[:, pcol0:pcol0 + ncols], lhsT=HT[:, k, :], rhs=WIN[:, k, col0:col0 + ncols],
                                            start=(k == 0), stop=(k == 7)),
              reads=[bHT, T("WIN")], writes=[psbuf])

    def proj_fm(HT, bHT, col0, ps, psbuf, pcol0=0):
        for k in range(8):
            E("pe", lambda e, k=k: e.matmul(ps[:, pcol0:pcol0 + 128], lhsT=WIN[:, k, col0:col0 + 128], rhs=HT[:, k, :],
                                            start=(k == 0), stop=(k == 7)),
              reads=[bHT, T("WIN")], writes=[psbuf])

    load_w_in(0, 4096, "A")
    ssq8, b_ssq8 = small(8)
    rs8, b_rs8 = small(8)

    def A_xn(c):
        if c + 1 < NCH:
            x_load(c + 1)
        x_norm_T(c, HTA, T("HTA"))

    def A_S1(c):
        p = c % 2
        QT, KT, VC, SG, SC = QT2[p], KT2[p], VC2[p], SG2[p], SC2[p]
        bVC, bSG = B("VC", p), B("SG", p)
        bHT = T("HTA")
        U = []

        def u_v(half):
            ps, pb_ = pbankA(False)
            proj_tm(HTA, bHT, 2048 + half * 512, 512, ps, pb_)
            E("dve", lambda e: e.tensor_copy(out=VC[:, half * 512:(half + 1) * 512], in_=ps[:, :]), reads=[pb_], writes=[bVC])

        def u_g(half):
            ps, pb_ = pbankA(False)
            proj_tm(HTA, bHT, 3072 + half * 512, 512, ps, pb_)
            silu_from_psum(SG[:, half * 512:(half + 1) * 512], bSG, ps[:, :], pb_)
        for half in range(2):
            U.append(lambda half=half: u_v(half))
        for half in range(2):
            U.append(lambda half=half: u_g(half))

        def grp(g):
            hs = g * 4
            if g == 0:
                te, tsp, tl, tb, tq = TTA
                be, bsp, bl, bb, bq = [T(f"TTA{i}") for i in range(5)]
            else:
                te, tsp, tl = TTB
                be, bsp, bl = [T(f"TTB{i}") for i in range(3)]
                tb, tq = F32T[0][:, 0:512], F32T[0][:, 512:1024]
                bb, bq = B("F0a"), B("F0b")
                te, tsp, tl = te[:, :], tsp[:, :], tl[:, :]
            if g == 0:
                te, tsp, tl, tb, tq = te[:, :], tsp[:, :], tl[:, :], tb[:, :], tq[:, :]
            bsc = B("SC", p, g)
            st = {}
            v4 = lambda ap: ap.rearrange("p (h t) -> p h t", h=4)

            def uA():
                psq, pbq = pbankA(True)
                psf, pbf = pbankA(False)
                st["psq"], st["pbq"] = psq, pbq
                for hh in range(4):
                    proj_fm(HTA, bHT, (hs + hh) * 128, psq, pbq, hh * 128)
                for hh in range(4):
                    proj_fm(HTA, bHT, 1024 + (hs + hh) * 128, psf, pbf, hh * 128)
                E("act", lambda e: e.activation(out=te, in_=psf[:, :], func=AF.Exp, scale=-1.0), reads=[pbf], writes=[be])
                E("act", lambda e: e.activation(out=tq, in_=psq[:, :], func=AF.Exp, scale=-1.0), reads=[pbq], writes=[bq])
                E("act", lambda e: e.activation(out=tsp, in_=te, func=AF.Ln, bias=1.0), reads=[be], writes=[bsp])
                E("act", lambda e: e.activation(out=tq, in_=tq, func=AF.Ln, bias=1.0), reads=[bq], writes=[bq])
                for hh in range(4):
                    h = hs + hh
                    E("act", lambda e, hh=hh, h=h: e.activation(out=tl[:, hh * 128:(hh + 1) * 128], in_=te[:, hh * 128:(hh + 1) * 128], func=AF.Ln,
                                                                scale=LBT[:, h:h + 1], bias=1.0), reads=[be, T("LBT")], writes=[bl])

            def uB():
                E("dve", lambda e: e.tensor_sub(out=tl, in0=tl, in1=tsp), reads=[bsp, bl], writes=[bl])
                E("dve", lambda e: e.tensor_tensor_scan(out=tb, data0=RES1[:, :], data1=tl, initial=0.0, op0=ALU.mult, op1=ALU.add),
                  reads=[bl, T("RES1")], writes=[bb])
                b63 = v4(tb)[:, :, 63]
                b127 = v4(tb)[:, :, 127]
                E("dve", lambda e: e.tensor_scalar(out=SC[:, 0, hs:hs + 4], in0=b63, scalar1=-1.0, scalar2=1.0, op0=ALU.mult, op1=ALU.mult), reads=[bb], writes=[bsc])
                E("dve", lambda e: e.tensor_sub(out=SC[:, 4, hs:hs + 4], in0=b127, in1=b63), reads=[bb, bsc], writes=[bsc])
                E("dve", lambda e: e.tensor_add(out=SC[:, 5, hs:hs + 4], in0=b63, in1=LBT[:, 16 + hs:20 + hs]), reads=[bb, bsc, T("LBT")], writes=[bsc])
                E("act", lambda e: e.activation(out=SC[:, 1, hs:hs + 4], in_=b63, func=AF.Exp), reads=[bb, bsc], writes=[bsc])
                E("act", lambda e: e.activation(out=SC[:, 2, hs:hs + 4], in_=b127, func=AF.Exp), reads=[bb, bsc], writes=[bsc])
                E("act", lambda e: e.activation(out=SC[:, 3, hs:hs + 4], in_=SC[:, 4, hs:hs + 4], func=AF.Exp), reads=[bsc], writes=[bsc])
                E("dve", lambda e: e.tensor_add(out=tsp, in0=tsp, in1=tb), reads=[bsp, bb], writes=[bsp])
                E("dve", lambda e: e.tensor_sub(out=tq, in0=tb, in1=tq), reads=[bq, bb], writes=[bq])

            def uC():
                psq, pbq = st["psq"], st["pbq"]
                for hh in range(4):
                    h = hs + hh
                    E("act", lambda e, hh=hh, h=h: e.activation(out=tsp[:, hh * 128:(hh + 1) * 128], in_=tsp[:, hh * 128:(hh + 1) * 128], func=AF.Exp, scale=-1.0,
                                                                bias=SC[:, 5, h:h + 1]), reads=[bsp, bsc], writes=[bsp])
                for hh in range(4):
                    h = hs + hh
                    E("act", lambda e, hh=hh, h=h: e.activation(out=tq[:, hh * 128:(hh + 1) * 128], in_=tq[:, hh * 128:(hh + 1) * 128], func=AF.Exp,
                                                                bias=SC[:, 0, h:h + 1]), reads=[bq, bsc], writes=[bq])
                E("pool", lambda e: e.tensor_tensor(out=KT[:, hs:hs + 4, :], in0=v4(te), in1=v4(tsp), op=ALU.mult), reads=[be, bsp], writes=[B("KT", p, g)])
                E("dve", lambda e: e.tensor_tensor(out=QT[:, hs:hs + 4, :], in0=v4(psq[:, :]), in1=v4(tq), op=ALU.mult), reads=[pbq, bq], writes=[B("QT", p, g)])
            return uA, uB, uC
        a0_, b0_, c0_ = grp(0)
        a1_, b1_, c1_ = grp(1)
        U += [a0_, a1_, b0_, b1_]
        if c + 1 < NCH:
            U.append(lambda: A_xn(c + 1))
        U += [c0_, c1_]
        return U

    def A_S2(c):
        p = c % 2
        QT, KT, VC, SG, SC = QT2[p], KT2[p], VC2[p], SG2[p], SC2[p]
        bVC, bSG = B("VC", p), B("SG", p)
        O32 = F32T[1]
        bO = T("F32T1")
        V = []

        def v_att(hb):
            ps, pb_ = pbank(2)
            for hh in range(4):
                h = hb * 4 + hh
                E("pe", lambda e, hh=hh, h=h: e.matmul(ps[:, hh * 128:(hh + 1) * 128], lhsT=KT[:, h, :], rhs=QT[:, h, :], start=True, stop=True),
                  reads=[B("KT", p, h // 4), B("QT", p, h // 4)], writes=[pb_])
            E("dve", lambda e: e.tensor_tensor(out=ATT[:, hb * 4:(hb + 1) * 4, :], in0=ps[:, :].rearrange("p (h t) -> p h t", h=4),
                                               in1=maskB[:, :].unsqueeze(1).to_broadcast([128, 4, 128]), op=ALU.mult),
              reads=[pb_, T("maskB")], writes=[B("ATT", hb)])

        def v_ktt():
            for h in range(8):
                E("pe", lambda e, h=h: e.transpose(out=PSB2[1][:, h * 128:(h + 1) * 128], in_=KT[:, h, :], identity=identB[:]),
                  reads=[B("KT", p, h // 4), T("identB")], writes=[bPSB2[1]])
            E("act", lambda e: e.activation(out=KTT[:, :, :], in_=PSB2[1][:, :].rearrange("p (k t) -> p k t", k=8), func=AF.Copy), reads=[bPSB2[1]], writes=[T("KTT")])

        def v_sbf():
            for h in range(8):
                E("dve", lambda e, h=h: e.tensor_scalar(out=SBF[:, h, :], in0=SST[:, h, :], scalar1=SC[:, 1, h:h + 1], scalar2=1.0, op0=ALU.mult, op1=ALU.mult),
                  reads=[T("SST"), B("SC", p, h // 4)], writes=[B("SBF", h)])

        def v_o(hb):
            ps, pb_ = pbank(2)
            for hh in range(4):
                h = hb * 4 + hh
                E("pe", lambda e, hh=hh, h=h: e.matmul(ps[:, hh * 128:(hh + 1) * 128], lhsT=ATT[:, h, :], rhs=VC[:, h * 128:(h + 1) * 128], start=True, stop=False),
                  reads=[B("ATT", hb), bVC], writes=[pb_])
                E("pe", lambda e, hh=hh, h=h: e.matmul(ps[:, hh * 128:(hh + 1) * 128], lhsT=QT[:, h, :], rhs=SBF[:, h, :], start=False, stop=True),
                  reads=[B("QT", p, h // 4), B("SBF", h)], writes=[pb_])
            E("act", lambda e: e.activation(out=O32[:, hb * 512:(hb + 1) * 512], in_=ps[:, :], func=AF.Copy), reads=[pb_], writes=[bO])

        def v_kv(hb):
            ps, pb_ = pbank(2)
            for hh in range(4):
                h = hb * 4 + hh
                E("pe", lambda e, hh=hh, h=h: e.matmul(ps[:, hh * 128:(hh + 1) * 128], lhsT=KTT[:, h, :], rhs=VC[:, h * 128:(h + 1) * 128], start=True, stop=True),
                  reads=[T("KTT"), bVC], writes=[pb_])
            for hh in range(4):
                h = hb * 4 + hh
                tk = TKV[h % 2]
                btk = T(f"TKV{h % 2}")
                E("dve", lambda e, hh=hh, h=h, tk=tk: e.tensor_scalar(out=tk[:], in0=ps[:, hh * 128:(hh + 1) * 128], scalar1=SC[:, 3, h:h + 1], scalar2=1.0,
                                                                      op0=ALU.mult, op1=ALU.mult),
                  reads=[pb_, B("SC", p, h // 4)], writes=[btk])
                E("dve", lambda e, h=h, tk=tk: e.scalar_tensor_tensor(out=SST[:, h, :], in0=SST[:, h, :], scalar=SC[:, 2, h:h + 1], in1=tk[:], op0=ALU.mult, op1=ALU.add),
                  reads=[btk, B("SC", p, h // 4), T("SST"), B("SBF", h)], writes=[T("SST")])

        def v_norm():
            E("dve", lambda e: e.tensor_mul(out=MIXBA[:], in0=O32[:], in1=O32[:]), reads=[bO], writes=[T("MIXBA")])
            E("dve", lambda e: e.tensor_reduce(out=ssq8, in_=MIXBA[:, :].rearrange("p (h v) -> p h v", h=8), axis=AX.X, op=ALU.add), reads=[T("MIXBA")], writes=[b_ssq8])
            rstd_from_ssq(ssq8, b_ssq8, 128.0, rs8, b_rs8)
            E("dve", lambda e: e.tensor_tensor(out=O32[:, :].rearrange("p (h v) -> p h v", h=8), in0=O32[:, :].rearrange("p (h v) -> p h v", h=8),
                                               in1=rs8.unsqueeze(2).to_broadcast([128, 8, 128]), op=ALU.mult), reads=[bO, b_rs8], writes=[bO])
            E("dve", lambda e: e.tensor_mul(out=MIXBA[:], in0=O32[:], in1=SG[:]), reads=[bO, bSG], writes=[T("MIXBA")])

        def v_mixt():
            for k in range(8):
                E("pe", lambda e, k=k: e.transpose(out=PSB2[1][:, k * 128:(k + 1) * 128], in_=MIXBA[:, k * 128:(k + 1) * 128], identity=identB[:]),
                  reads=[T("MIXBA"), T("identB")], writes=[bPSB2[1]])
            E("act", lambda e: e.activation(out=MIXT[:, :, c * 128:(c + 1) * 128], in_=PSB2[1][:, :].rearrange("p (k t) -> p k t", k=8), func=AF.Copy),
              reads=[bPSB2[1]], writes=[T("MIXT")])

        V += [lambda: v_att(0), lambda: v_att(1), v_ktt, v_sbf, lambda: v_o(0), lambda: v_o(1), lambda: v_kv(0), lambda: v_kv(1), v_norm, v_mixt]
        return V

    x_load(0)
    A_xn(0)
    for u in A_S1(0):
        u()
    for c in range(NCH):
        U = A_S1(c + 1) if c + 1 < NCH else []
        V = A_S2(c)
        if U:
            zipper(U, V)
        else:
            for v in V:
                v()

    replay_all()
    while len(stack) > n_common:
        stack.pop().__exit__(None, None, None)
    WIN = sb("WINB", [128, 8, 2576], BF16)
    WOUT = sb("WOUT", [128, 16, 1024], BF16)
    HTB = [sb(f"HTB{i}", [128, 8, 128], BF16) for i in range(2)]
    XBCT = sb("XBCT", [128, 12, 131], BF16)
    XSB2 = [sb(f"XSB{i}", [128, 1024], BF16) for i in range(2)]
    BTM2 = [sb(f"BTM{i}", [128, 256], BF16) for i in range(2)]
    BCT2 = [sb(f"BCT{i}", [128, 4, 128], BF16) for i in range(2)]
    CS2T2 = [sb(f"CS2T{i}", [48, 128], BF16) for i in range(2)]
    NCS2T2 = [sb(f"NCS2T{i}", [48, 128], BF16) for i in range(2)]
    XDT = sb("XDT", [128, 1024], BF16)
    XDD = sb("XDD", [128, 1024], BF16)
    XSD = sb("XSD", [128, 1024], BF16)
    LT = sb("LT", [128, 16, 128], BF16)
    CBM = sb("CBM", [128, 2, 128], BF16)
    ST32 = sb("ST32", [128, 1024], F32)
    STB = sb("STB", [128, 1024], BF16)
    MIXT2 = sb("MIXT2", [128, 8, 128], BF16)
    DG = sb("DG", [128, 8, 128], BF16)
    SEL = sb("SEL", [48, 16, 128], BF16)
    HI48 = sb("HI48", [48, 128], BF16)
    APADH = sb("APADH", [128, 48], BF16)
    APADL = sb("APADL", [128, 48], BF16)
    onesB = sb("onesB", [128, 128], BF16)
    NEGM = sb("NEGM", [128, 128], BF16)
    CBROW = sb("CBROW", [1, 1280], BF16)
    ONESROW = sb("ONESROW", [1, 128], BF16)
    print("sbuf bytes remaining before TMPB:", nc.sbuf_bytes_remaining)
    TMPB = sb("TMPB", [128, 2, 256], F32)
    print("sbuf bytes remaining after pass B alloc:", nc.sbuf_bytes_remaining)
    tmpb_rr = [0]

    def tmpb(n):
        i = tmpb_rr[0] % 2
        tmpb_rr[0] += 1
        return TMPB[:, i, 0:n], B("TMPB", i)
    P.barrier()
    E("pool", lambda e: e.memset(ST32[:], 0.0), writes=[T("ST32")])
    E("pool", lambda e: e.memset(STB[:], 0.0), writes=[T("STB")])
    E("pool", lambda e: e.memset(XBCT[:], 0.0), writes=[T("XBCT")])
    for i in range(2):
        E("pool", lambda e, i=i: e.memset(CS2T2[i][:], 0.0), writes=[B("CS2T", i)])
    E("pool", lambda e: e.memset(ONESROW[:], 1.0), writes=[T("ONESROW")])
    E("pool", lambda e: e.memset(APADH[:], 0.0), writes=[T("APADH")])
    E("pool", lambda e: e.memset(APADL[:], 0.0), writes=[T("APADL")])
    E("pool", lambda e: e.memset(onesB[:], 1.0), writes=[T("onesB")])
    E("pool", lambda e: e.memset(SEL[:], 0.0), writes=[T("SEL")])
    for base in (0, 32):
        E("pool", lambda e, base=base: e.affine_select(
            out=SEL[base:base + 16, :, :], in_=SEL[base:base + 16, :, :], pattern=[[-1, 16], [0, 128]],
            compare_op=ALU.not_equal, fill=1.0, base=0, channel_multiplier=1), reads=[T("SEL")], writes=[T("SEL")])
    E("pool", lambda e: e.tensor_scalar(out=NEGM[:, :], in0=triF[:], scalar1=-1.0, scalar2=30000.0, op0=ALU.add, op1=ALU.mult),
      reads=[T("triF")], writes=[T("NEGM")])
    load_w_in(4096, 2576, "B")
    load_w_out()
    P.dma(lambda e: e.dma_start(out=F32T[0][0:1, 0:1024], in_=pb_d[0:1, 1072:2096]), "stg0", writes=[T("F32T0")])
    P.dma(lambda e: e.dma_start(out=F32T[1][0:1, 0:256], in_=pb_d[0:1, 2096:2352]), "stg1", writes=[T("F32T1")])
    E("dve", lambda e: e.tensor_copy(out=CBROW[0:1, 0:1024], in_=F32T[0][0:1, 0:1024]), reads=[T("F32T0")], writes=[T("CBROW")])
    E("dve", lambda e: e.tensor_copy(out=CBROW[0:1, 1024:1280], in_=F32T[1][0:1, 0:256]), reads=[T("F32T1")], writes=[T("CBROW")])
    Abc, bA = small(16)
    E("act", lambda e: e.activation(out=Abc, in_=PBT[:, 16:32], func=AF.Exp), reads=[T("PBT")], writes=[bA])
    E("dve", lambda e: e.tensor_scalar(out=Abc, in0=Abc, scalar1=-1.0, scalar2=1.0, op0=ALU.mult, op1=ALU.mult), reads=[bA], writes=[bA])
    ncb, b_ncb = small(12)
    E("dve", lambda e: e.tensor_scalar(out=ncb, in0=PPT[:, 88:100], scalar1=-1.0, scalar2=1.0, op0=ALU.mult, op1=ALU.mult), reads=[T("PPT")], writes=[b_ncb])
    av, b_a = small(16)
    cssb, b_cs = small(32)
    dtv2 = [small(16) for _ in range(2)]
    ecs2 = [small(16) for _ in range(2)]
    ddv2 = [small(16) for _ in range(2)]
    etot2 = [small(16) for _ in range(2)]
    ssq2, b_ssq2 = small(2)
    rs2, b_rs2 = small(2)
    ssqf, b_ssqf = small(1)
    rsf, b_rsf = small(1)
    dg_rr = [0]

    def B_xn(c):
        if c + 1 < NCH:
            x_load(c + 1)
        x_norm_T(c, HTB[c % 2], B("HTB", c % 2))

    def B_S1(c):
        p = c % 2
        HT, bHT = HTB[p], B("HTB", p)
        XSB, BTM, BCT, CS2T, NCS2T = XSB2[p], BTM2[p], BCT2[p], CS2T2[p], NCS2T2[p]
        dtv, b_dt = dtv2[p]
        ecs, b_ecs = ecs2[p]
        ddv, b_dd = ddv2[p]
        etot, b_etot = etot2[p]
        U = []

        def u_first():
            gen_dg(jorder[0])
            gen_dg(jorder[1])
            B_xn(c)
        U.append(u_first)

        xb_st = {}

        def u_xbc(jb, jj):
            if jj == 0:
                xb_st[jb] = pbank(1)
            ps, pb_ = xb_st[jb]
            proj_fm(HT, bHT, 1024 + (jb * 4 + jj) * 128, ps, pb_, jj * 128)
            if jj == 3:
                E("dve", lambda e: e.tensor_copy(out=XBCT[:, jb * 4:(jb + 1) * 4, 3:131], in_=ps[:, :].rearrange("p (j t) -> p j t", j=4)),
                  reads=[pb_], writes=[T("XBCT")])
        for jb in range(3):
            for jj in range(4):
                U.append(lambda jb=jb, jj=jj: u_xbc(jb, jj))

        def u_dt():
            ps_dt, pb_dt = pbank(1)
            proj_tm(HT, bHT, 2560, 16, ps_dt, pb_dt)
            E("dve", lambda e: e.tensor_tensor(out=dtv, in0=ps_dt[:, 0:16], in1=PBT[:, 0:16], op=ALU.add), reads=[pb_dt, T("PBT")], writes=[b_dt])
            E("act", lambda e: e.activation(out=dtv, in_=dtv, func=AF.Exp), reads=[b_dt], writes=[b_dt])
            E("act", lambda e: e.activation(out=dtv, in_=dtv, func=AF.Ln, bias=1.0), reads=[b_dt], writes=[b_dt])
            E("dve", lambda e: e.tensor_mul(out=av, in0=dtv, in1=Abc), reads=[b_dt, bA], writes=[b_a])
            E("dve", lambda e: e.tensor_copy(out=APADH[:, 0:16], in_=av), reads=[b_a, T("APADH")], writes=[T("APADH")])
            E("dve", lambda e: e.tensor_sub(out=APADL[:, 0:16], in0=av, in1=APADH[:, 0:16]), reads=[b_a, T("APADH"), T("APADL")], writes=[T("APADL")])
            E("dve", lambda e: e.tensor_copy(out=APADH[:, 32:48], in_=APADH[:, 0:16]), reads=[T("APADH")], writes=[T("APADH")])
            E("dve", lambda e: e.tensor_copy(out=APADL[:, 32:48], in_=APADL[:, 0:16]), reads=[T("APADL")], writes=[T("APADL")])
            ps_s, pb_s = pbank(1)
            for i_, ap_ in enumerate((APADH, APADL)):
                E("pe", lambda e, i_=i_, ap_=ap_: e.matmul(ps_s[:, 0:16], lhsT=maskB[:], rhs=ap_[:, 0:16], start=(i_ == 0), stop=(i_ == 1)),
                  reads=[T("maskB"), T("APADH"), T("APADL")], writes=[pb_s])
            for i_, ap_ in enumerate((APADH, APADL)):
                E("pe", lambda e, i_=i_, ap_=ap_: e.matmul(ps_s[:, 16:32], lhsT=onesB[:], rhs=ap_[:, 0:16], start=(i_ == 0), stop=(i_ == 1)),
                  reads=[T("onesB"), T("APADH"), T("APADL")], writes=[pb_s])
            for i_, ap_ in enumerate((APADH, APADL)):
                E("pe", lambda e, i_=i_, ap_=ap_: e.matmul(ps_s[0:48, 128:256], lhsT=ap_[:, 0:48], rhs=maskB[:], start=(i_ == 0), stop=(i_ == 1)),
                  reads=[T("maskB"), T("APADH"), T("APADL")], writes=[pb_s])
            E("dve", lambda e: e.tensor_copy(out=cssb, in_=ps_s[:, 0:32]), reads=[pb_s], writes=[b_cs])
            E("act", lambda e: e.activation(out=ecs, in_=cssb[:, 0:16], func=AF.Exp), reads=[b_cs], writes=[b_ecs])
            E("act", lambda e: e.activation(out=etot, in_=cssb[:, 16:32], func=AF.Exp), reads=[b_cs], writes=[b_etot])
            E("dve", lambda e: e.tensor_sub(out=ddv, in0=cssb[:, 16:32], in1=cssb[:, 0:16]), reads=[b_cs], writes=[b_dd])
            E("act", lambda e: e.activation(out=ddv, in_=ddv, func=AF.Exp), reads=[b_dd], writes=[b_dd])
            E("dve", lambda e: e.tensor_mul(out=ddv, in0=ddv, in1=dtv), reads=[b_dd, b_dt], writes=[b_dd])
            E("dve", lambda e: e.tensor_copy(out=HI48[:, :], in_=ps_s[0:48, 128:256]), reads=[pb_s], writes=[T("HI48")])
            E("dve", lambda e: e.tensor_copy(out=CS2T[0:16, :], in_=HI48[0:16, :]), reads=[T("HI48")], writes=[B("CS2T", p)])
            E("dve", lambda e: e.tensor_sub(out=CS2T[32:48, :], in0=ps_s[32:48, 128:256], in1=HI48[32:48, :]), reads=[pb_s, T("HI48"), B("CS2T", p)], writes=[B("CS2T", p)])
            E("dve", lambda e: e.tensor_scalar(out=NCS2T[:, :], in0=CS2T[:, :], scalar1=-1.0, scalar2=1.0, op0=ALU.mult, op1=ALU.mult), reads=[B("CS2T", p)], writes=[B("NCS2T", p)])
        U.append(u_dt)

        conv = {"dg": {}}
        jorder = (8, 0, 1, 9, 2, 3, 10, 4, 5, 11, 6, 7)

        def gen_dg(j):
            dgs = []
            for tap in range(4):
                sl = dg_rr[0] % 8
                dg_rr[0] += 1
                bd = B("DG", sl)
                wcol = PPT[:, 40 + j * 4 + tap:41 + j * 4 + tap]
                E("pool", lambda e, sl=sl, wcol=wcol: e.tensor_scalar(out=DG[:, sl, :], in0=identB[:], scalar1=wcol, scalar2=1.0, op0=ALU.mult, op1=ALU.mult),
                  reads=[T("identB"), T("PPT")], writes=[bd])
                dgs.append((sl, bd))
            conv["dg"][j] = dgs

        def u_conv(j):
            if "ps_x" not in conv:
                conv["ps_x"] = [pbank(1), pbank(1)]
                conv["ps_b"] = pbank(1)
                conv["ps_f"] = pbank(1)
            dgs = conv["dg"].pop(j)
            for tap in range(0):
                sl = dg_rr[0] % 8
                dg_rr[0] += 1
                bd = B("DG", sl)
                wcol = PPT[:, 40 + j * 4 + tap:41 + j * 4 + tap]
                geng = "pool"
                if geng == "act":
                    E("act", lambda e, sl=sl, wcol=wcol: e.activation(out=DG[:, sl, :], in_=identB[:], func=AF.Copy, scale=wcol), reads=[T("identB"), T("PPT")], writes=[bd])
                else:
                    E(geng, lambda e, sl=sl, wcol=wcol: e.tensor_scalar(out=DG[:, sl, :], in0=identB[:], scalar1=wcol, scalar2=1.0, op0=ALU.mult, op1=ALU.mult),
                      reads=[T("identB"), T("PPT")], writes=[bd])
                dgs.append((sl, bd))
            if j < 10:
                if j < 8:
                    ps, pb_ = conv["ps_x"][j // 4]
                    pc = (j % 4) * 128
                else:
                    ps, pb_ = conv["ps_b"]
                    pc = (j - 8) * 128
                for tap in range(4):
                    sl, bd = dgs[tap]
                    E("pe", lambda e, tap=tap, sl=sl: e.matmul(ps[:, pc:pc + 128], lhsT=XBCT[:, j, tap:tap + 128], rhs=DG[:, sl, :],
                                                               start=(tap == 0), stop=False),
                      reads=[T("XBCT"), bd], writes=[pb_])
                E("pe", lambda e: e.matmul(ps[:, pc:pc + 128], lhsT=ONESROW[0:1, :], rhs=CBROW[0:1, j * 128:(j + 1) * 128], start=False, stop=True),
                  reads=[T("ONESROW"), T("CBROW")], writes=[pb_])
            if j >= 8:
                ps, pb_ = conv["ps_f"]
                pc = (j - 8) * 128
                for tap in range(4):
                    sl, bd = dgs[tap]
                    E("pe", lambda e, tap=tap, sl=sl: e.matmul(ps[:, pc:pc + 128], lhsT=DG[:, sl, :], rhs=XBCT[:, j, tap:tap + 128],
                                                               start=(tap == 0), stop=(tap == 3)),
                      reads=[T("XBCT"), bd], writes=[pb_])
                tb, tbb = tmpb(128)
                E("act", lambda e: e.activation(out=tb, in_=ps[:, pc:pc + 128], func=AF.Exp, scale=-1.0, bias=ncb[:, j:j + 1]),
                  reads=[pb_, b_ncb], writes=[tbb])
                E("act", lambda e: e.activation(out=tb, in_=tb, func=AF.Ln, bias=1.0), reads=[tbb], writes=[tbb])
                E("act", lambda e: e.activation(out=tb, in_=tb, func=AF.Exp, scale=-1.0), reads=[tbb], writes=[tbb])
                E("dve", lambda e: e.scalar_tensor_tensor(out=BCT[:, j - 8, :], in0=ps[:, pc:pc + 128], scalar=PPT[:, 88 + j:89 + j], in1=tb,
                                                          op0=ALU.add, op1=ALU.mult),
                  reads=[pb_, T("PPT"), tbb], writes=[B("BCT", p, j - 8)])
        def u_conv_i(i):
            u_conv(jorder[i])
            if i + 2 < 12:
                gen_dg(jorder[i + 2])
        for i in range(12):
            U.append(lambda i=i: u_conv_i(i))

        def u_xs():
            E("dve", lambda e: e.tensor_copy(out=XBCT[:, :, 0:3], in_=XBCT[:, :, 128:131]), reads=[T("XBCT")], writes=[T("XBCT")])
            for half in range(2):
                ps, pb_ = conv["ps_x"][half]
                for qq in range(2):
                    tb, tbb = tmpb(256)
                    silu_from_psum(XSB[:, half * 512 + qq * 256:half * 512 + (qq + 1) * 256], B("XSB", p), ps[:, qq * 256:(qq + 1) * 256], pb_, tmp=tb, tbuf=tbb)
            tb, tbb = tmpb(256)
            silu_from_psum(BTM[:, :], B("BTM", p), conv["ps_b"][0][:, 0:256], conv["ps_b"][1], tmp=tb, tbuf=tbb)
        U.append(u_xs)
        return U

    def B_S2(c):
        p = c % 2
        HT, bHT = HTB[p], B("HTB", p)
        XSB, BTM, BCT, CS2T, NCS2T = XSB2[p], BTM2[p], BCT2[p], CS2T2[p], NCS2T2[p]
        dtv, b_dt = dtv2[p]
        ecs, b_ecs = ecs2[p]
        ddv, b_dd = ddv2[p]
        etot, b_etot = etot2[p]
        sl = c % 2
        SZ, bSZ = F32T[sl], T(f"F32T{sl}")
        Y32, bY = F32T[1 - sl], T(f"F32T{1 - sl}")
        OUT, bOUT = SZ, bSZ
        V = []

        def v_z(half):
            ps, pb_ = pbank(2)
            proj_tm(HT, bHT, half * 512, 512, ps, pb_)
            silu_from_psum(SZ[:, half * 512:(half + 1) * 512], bSZ, ps[:, :], pb_)
        V += [lambda: v_z(0), lambda: v_z(1)]

        def v_scale():
            v3 = lambda t: t[:, :].rearrange("p (h q) -> p h q", h=16)
            bc3 = lambda v: v.unsqueeze(2).to_broadcast([128, 16, 64])
            E("dve", lambda e: e.tensor_tensor(out=v3(XDT), in0=v3(XSB), in1=bc3(dtv), op=ALU.mult), reads=[B("XSB", p), b_dt], writes=[T("XDT")])
            E("dve", lambda e: e.tensor_tensor(out=v3(XDD), in0=v3(XSB), in1=bc3(ddv), op=ALU.mult), reads=[B("XSB", p), b_dd], writes=[T("XDD")])
            E("dve", lambda e: e.tensor_tensor(out=v3(XSD), in0=v3(XSB), in1=bc3(PBT[:, 32:48]), op=ALU.mult), reads=[B("XSB", p), T("PBT")], writes=[T("XSD")])
        V.append(v_scale)

        def v_cb():
            ps, pb_ = pbank(2)
            for g in range(2):
                E("pe", lambda e, g=g: e.matmul(ps[:, g * 128:(g + 1) * 128], lhsT=BCT[:, g, :], rhs=BCT[:, 2 + g, :], start=True, stop=True),
                  reads=[B("BCT", p, g), B("BCT", p, 2 + g)], writes=[pb_])
            E("dve", lambda e: e.tensor_tensor(out=CBM[:, :, :], in0=ps[:, 0:256].rearrange("p (g t) -> p g t", g=2),
                                               in1=maskB[:, :].unsqueeze(1).to_broadcast([128, 2, 128]), op=ALU.mult), reads=[pb_, T("maskB")], writes=[T("CBM")])
        V.append(v_cb)

        def v_seg(hb):
            ps, pb_ = pbank(2)
            for hh in range(4):
                h = hb * 4 + hh
                E("pe", lambda e, hh=hh: e.matmul(ps[:, hh * 128:(hh + 1) * 128], lhsT=identB[:], rhs=NEGM[:, :], start=True, stop=False),
                  reads=[T("identB"), T("NEGM")], writes=[pb_])
                E("pe", lambda e, hh=hh, h=h: e.matmul(ps[:, hh * 128:(hh + 1) * 128], lhsT=SEL[0:48, h, :], rhs=CS2T[0:48, :], start=False, stop=False),
                  reads=[T("SEL"), B("CS2T", p)], writes=[pb_])
                E("pe", lambda e, hh=hh, h=h: e.matmul(ps[:, hh * 128:(hh + 1) * 128], lhsT=NCS2T[0:48, :], rhs=SEL[0:48, h, :], start=False, stop=True),
                  reads=[T("SEL"), B("NCS2T", p)], writes=[pb_])
            E("act", lambda e: e.activation(out=LT[:, hb * 4:(hb + 1) * 4, :], in_=ps[:, :].rearrange("p (h t) -> p h t", h=4), func=AF.Exp),
              reads=[pb_], writes=[B("LT", hb // 2)])
        for hb in range(4):
            V.append(lambda hb=hb: v_seg(hb))

        def v_y(g):
            E("dve", lambda e: e.scalar_tensor_tensor(out=LT[:, g * 8:(g + 1) * 8, :], in0=LT[:, g * 8:(g + 1) * 8, :], scalar=1.0,
                                                      in1=CBM[:, g, :].unsqueeze(1).to_broadcast([128, 8, 128]), op0=ALU.min, op1=ALU.mult),
              reads=[B("LT", g), T("CBM")], writes=[B("LT", g)])
            psa, pba = pbank(2)
            psy, pby = pbank(2)
            E("pe", lambda e: e.matmul(psa[:, :], lhsT=identB[:], rhs=XSD[:, g * 512:(g + 1) * 512], start=True, stop=False),
              reads=[T("identB"), T("XSD")], writes=[pba])
            for hh in range(8):
                h = g * 8 + hh
                E("pe", lambda e, hh=hh, h=h: e.matmul(psa[:, hh * 64:(hh + 1) * 64], lhsT=LT[:, h, :], rhs=XDT[:, h * 64:(h + 1) * 64], start=False, stop=(hh == 7)),
                  reads=[B("LT", g), T("XDT")], writes=[pba])
            E("pe", lambda e: e.matmul(psy[:, :], lhsT=BCT[:, 2 + g, :], rhs=STB[:, g * 512:(g + 1) * 512], start=True, stop=True),
              reads=[B("BCT", p, 2 + g), T("STB")], writes=[pby])
            E("dve", lambda e: e.tensor_tensor(out=Y32[:, g * 512:(g + 1) * 512].rearrange("p (h q) -> p h q", h=8),
                                               in0=psy[:, :].rearrange("p (h q) -> p h q", h=8),
                                               in1=ecs[:, g * 8:(g + 1) * 8].unsqueeze(2).to_broadcast([128, 8, 64]), op=ALU.mult),
              reads=[pby, b_ecs], writes=[bY])
            E("dve", lambda e: e.tensor_tensor(out=Y32[:, g * 512:(g + 1) * 512], in0=psa[:, :], in1=Y32[:, g * 512:(g + 1) * 512], op=ALU.add),
              reads=[pba, bY], writes=[bY])
        V += [lambda: v_y(0), lambda: v_y(1)]

        def v_state():
            for g in range(2):
                ps, pb_ = pbank(2)
                E("pe", lambda e, g=g, ps=ps: e.matmul(ps[:, :], lhsT=BTM[:, g * 128:(g + 1) * 128], rhs=XDD[:, g * 512:(g + 1) * 512], start=True, stop=True),
                  reads=[B("BTM", p), T("XDD")], writes=[pb_])
                sv = ST32[:, g * 512:(g + 1) * 512]
                E("dve", lambda e, sv=sv, g=g: e.tensor_tensor(out=sv.rearrange("p (h q) -> p h q", h=8), in0=sv.rearrange("p (h q) -> p h q", h=8),
                                                               in1=etot[:, g * 8:(g + 1) * 8].unsqueeze(2).to_broadcast([128, 8, 64]), op=ALU.mult),
                  reads=[T("ST32"), b_etot], writes=[T("ST32")])
                E("dve", lambda e, ps=ps, sv=sv: e.tensor_tensor(out=sv, in0=ps[:, :], in1=sv, op=ALU.add), reads=[pb_, T("ST32")], writes=[T("ST32")])
            E("act", lambda e: e.activation(out=STB[:, :], in_=ST32[:, :], func=AF.Copy), reads=[T("ST32")], writes=[T("STB")])
        V.append(v_state)

        def v_gate():
            E("dve", lambda e: e.tensor_mul(out=Y32[:], in0=Y32[:], in1=SZ[:]), reads=[bY, bSZ], writes=[bY])
            for g in range(2):
                E("act", lambda e, g=g: e.activation(out=XSD[:, g * 512:(g + 1) * 512], in_=Y32[:, g * 512:(g + 1) * 512], func=AF.Square, accum_out=ssq2[:, g:g + 1]),
                  reads=[bY], writes=[T("XSD"), b_ssq2])
            rstd_from_ssq(ssq2, b_ssq2, 512.0, rs2, b_rs2)
            for g in range(2):
                E("act", lambda e, g=g: e.activation(out=XSD[:, g * 512:(g + 1) * 512], in_=Y32[:, g * 512:(g + 1) * 512], func=AF.Copy, scale=rs2[:, g:g + 1]),
                  reads=[bY, b_rs2], writes=[T("XSD")])
            for k in range(8):
                E("pe", lambda e, k=k: e.transpose(out=PSB2[1][:, k * 128:(k + 1) * 128], in_=XSD[:, k * 128:(k + 1) * 128], identity=identB[:]),
                  reads=[T("XSD"), T("identB")], writes=[bPSB2[1]])
            E("act", lambda e: e.activation(out=MIXT2[:, :, :], in_=PSB2[1][:, :].rearrange("p (k t) -> p k t", k=8), func=AF.Copy), reads=[bPSB2[1]], writes=[T("MIXT2")])
            P.dma(lambda e: e.dma_start(out=OUT[:], in_=x_d[c * 128:(c + 1) * 128, :]), f"stg{sl}", writes=[bOUT])
        V.append(v_gate)

        def v_out(half):
            ps, pb_ = pbank(2)
            for k in range(16):
                if k < 8:
                    E("pe", lambda e, k=k: e.matmul(ps[:, :], lhsT=MIXT[:, k, c * 128:(c + 1) * 128], rhs=WOUT[:, k, half * 512:(half + 1) * 512],
                                                    start=(k == 0), stop=False), reads=[T("MIXT"), T("WOUT")], writes=[pb_])
                else:
                    E("pe", lambda e, k=k: e.matmul(ps[:, :], lhsT=MIXT2[:, k - 8, :], rhs=WOUT[:, k, half * 512:(half + 1) * 512],
                                                    start=False, stop=(k == 15)), reads=[T("MIXT2"), T("WOUT")], writes=[pb_])
            E("dve", lambda e: e.tensor_tensor(out=OUT[:, half * 512:(half + 1) * 512], in0=ps[:, :], in1=OUT[:, half * 512:(half + 1) * 512], op=ALU.add),
              reads=[pb_, bOUT], writes=[bOUT])
        TAIL = [lambda: v_out(0), lambda: v_out(1)]

        def v_final():
            E("act", lambda e: e.activation(out=XDT[:], in_=OUT[:], func=AF.Square, accum_out=ssqf), reads=[bOUT], writes=[T("XDT"), b_ssqf])
            rstd_from_ssq(ssqf, b_ssqf, 1024.0, rsf, b_rsf)
            E("dve", lambda e: e.scalar_tensor_tensor(out=OUT[:], in0=OUT[:], scalar=rsf, in1=PBT[:, 48:1072], op0=ALU.mult, op1=ALU.mult),
              reads=[bOUT, b_rsf, T("PBT")], writes=[bOUT])
            P.dma(lambda e: e.dma_start(out=y_d[c * 128:(c + 1) * 128, :], in_=OUT[:]), f"yout{sl}", reads=[bOUT])
        TAIL.append(v_final)
        return V, TAIL

    x_load(0)
    for u in B_S1(0):
        u()
    tail_prev = []
    for c in range(NCH):
        U = B_S1(c + 1) if c + 1 < NCH else []
        head, tail = B_S2(c)
        V = tail_prev + head
        tail_prev = tail
        if U:
            zipper(U, V, 0.15, 1.0)
        else:
            for v in V:
                v()
    for v in tail_prev:
        v()

    replay_all(final=True)
    for g in reversed(stack):
        g.__exit__(None, None, None)
    for g in reversed(pstack):
        g.__exit__(None, None, None)
    for g in reversed(semstack):
        g.__exit__(None, None, None)
    global LASTP
    LASTP = P
    return nc, dbg_d


def pack_params(norm_w, hg_lb_logits, hg_norm_w, m2_conv_w, m2_conv_b, m2_dt_bias, m2_a_log, m2_d_skip, m2_norm_w, final_norm_w):
    pp = np.zeros((128, NPP), np.float32)
    pp[:, 0:8] = norm_w.reshape(8, 128).T
    pp[:, 8:24] = np.concatenate([hg_norm_w.reshape(-1), m2_norm_w.reshape(-1)]).reshape(16, 128).T
    pp[:, 24:32] = hg_lb_logits[0].reshape(8, 128).T
    pp[:, 32:40] = hg_lb_logits[1].reshape(8, 128).T
    cw = m2_conv_w.reshape(4, 12, 128)
    pp[:, 40:88] = cw.transpose(2, 1, 0).reshape(128, 48)
    pp[:, 88:100] = m2_conv_b.reshape(12, 128).T
    pb = np.zeros((1, NPB), np.float32)
    pb[0, 0:16] = m2_dt_bias.reshape(-1)
    pb[0, 16:32] = m2_a_log.reshape(-1)
    pb[0, 32:48] = m2_d_skip.reshape(-1)
    pb[0, 48:1072] = final_norm_w.reshape(-1)
    pb[0, 1072:2608] = m2_conv_b.reshape(-1)
    return pp, pb


_CACHE = {}


def run(x, norm_w, w_in, hg_lb_logits, hg_norm_w, m2_conv_w, m2_conv_b, m2_dt_bias, m2_a_log, m2_d_skip, m2_norm_w, w_out,
        final_norm_w, dbg=()):
    x = np.asarray(x, np.float32)
    Bn, L, _ = x.shape
    nc, dbg_d = build(L, dbg)
    pp, pb = pack_params(np.asarray(norm_w[0]), np.asarray(hg_lb_logits), np.asarray(hg_norm_w[0]), np.asarray(m2_conv_w[0]),
                         np.asarray(m2_conv_b[0]), np.asarray(m2_dt_bias[0]), np.asarray(m2_a_log[0]), np.asarray(m2_d_skip[0]),
                         np.asarray(m2_norm_w[0]), np.asarray(final_norm_w))
    win = np.ascontiguousarray(np.asarray(w_in[0], np.float32))
    wout = np.ascontiguousarray(np.asarray(w_out[0], np.float32))
    in_maps = [{"x": np.ascontiguousarray(x[b]), "w_in": win, "w_out": wout, "pp": pp, "pb": pb} for b in range(Bn)]
    res = run_bass_kernel_spmd(nc, in_maps, core_ids=list(range(Bn)))
    out = np.stack([np.asarray(r["y"]) for r in res.results], axis=0).astype(np.float32)
    if dbg:
        return out, [{k: np.asarray(r["dbg_" + k]) for k in dbg_d} for r in res.results]
    return out


def kernel(x, norm_w, w_in, hg_lb_logits, hg_norm_w, m2_conv_w, m2_conv_b, m2_dt_bias, m2_a_log, m2_d_skip, m2_norm_w, w_out,
           final_norm_w):
    return run(x, norm_w, w_in, hg_lb_logits, hg_norm_w, m2_conv_w, m2_conv_b, m2_dt_bias, m2_a_log, m2_d_skip, m2_norm_w,
               w_out, final_norm_w)
```

```python
import numpy as np
import concourse.bass as bass
import concourse.mybir as mybir
from concourse.bass_utils import run_bass_kernel_spmd

F32 = mybir.dt.float32
BF16 = mybir.dt.bfloat16
AF = mybir.ActivationFunctionType
ALU = mybir.AluOpType
AX = mybir.AxisListType

D = 1024
EPS = 1e-6
NPP = 100
NPB = 2608
IN_COLS = 6672


class Buf:
    __slots__ = ("name", "w", "r")

    def __init__(self, name):
        self.name = name
        self.w = None
        self.r = {}


class _Rec:
    def __init__(self):
        self.call = None

    def __getattr__(self, name):
        def f(*a, **k):
            self.call = (name, a, k)
            return self
        return f


def _record(fn):
    r = _Rec()
    fn(r)
    assert r.call is not None
    return r.call


class Prog:
    ENGS = ("pe", "act", "dve", "pool", "sp")

    def __init__(self):
        self.ops = {e: [] for e in self.ENGS}
        self.cnt = {e: 0 for e in self.ENGS}
        self.known = {e: {} for e in self.ENGS}
        self.dmacnt = {}
        self.bufs = {}
        self.pending = {e: [] for e in self.ENGS}
        self.cursor = {e: 0 for e in self.ENGS}

    def barrier(self):
        for e in self.ENGS:
            kn = self.known[e]
            for k in list(self.ENGS[:4]) + list(self.dmacnt.keys()):
                v = self.cnt[k] if k in self.cnt else self.dmacnt[k]
                if k == e or v == 0 or kn.get(k, 0) >= v:
                    continue
                kn[k] = v
                self.pending[e].append((k, v))

    def buf(self, *key):
        b = self.bufs.get(key)
        if b is None:
            b = self.bufs[key] = Buf(key)
        return b

    def _waits(self, eng, reads, writes):
        deps = {}

        def add(k, v):
            if deps.get(k, 0) < v:
                deps[k] = v

        for b in reads:
            if b.w is not None:
                add(*b.w)
        for b in writes:
            if b.w is not None:
                add(*b.w)
            for k, v in b.r.items():
                add(k, v)
        waits = []
        kn = self.known[eng]
        for k, v in deps.items():
            if k == "pe" and eng == "pe":
                continue
            if kn.get(k, 0) >= v:
                continue
            kn[k] = v
            waits.append((k, v))
        if self.pending[eng]:
            waits = self.pending[eng] + waits
            self.pending[eng] = []
        return waits

    def emit(self, eng, fn, reads=(), writes=()):
        waits = self._waits(eng, reads, writes)
        self.cnt[eng] += 1
        idx = self.cnt[eng]
        self.ops[eng].append((waits, _record(fn), (eng, 1), [b.name for b in reads], [b.name for b in writes]))
        for b in writes:
            b.w = (eng, idx)
            b.r = {}
        for b in reads:
            if b.r.get(eng, 0) < idx:
                b.r[eng] = idx
        return idx

    def dma(self, fn, semkey, reads=(), writes=()):
        waits = self._waits("sp", reads, writes)
        self.dmacnt[semkey] = self.dmacnt.get(semkey, 0) + 16
        v = self.dmacnt[semkey]
        self.ops["sp"].append((waits, _record(fn), (semkey, 16), [b.name for b in reads], [b.name for b in writes]))
        for b in writes:
            b.w = (semkey, v)
            b.r = {}
        for b in reads:
            b.r[semkey] = v


def build(L, dbg=()):
    NCH = L // 128
    nc = bass.Bass("TRN2", target_bir_lowering=False)
    x_d = nc.dram_tensor("x", [L, D], F32, kind="ExternalInput").ap()
    win_d = nc.dram_tensor("w_in", [D, IN_COLS], F32, kind="ExternalInput").ap()
    wout_d = nc.dram_tensor("w_out", [2 * D, D], F32, kind="ExternalInput").ap()
    pp_d = nc.dram_tensor("pp", [128, NPP], F32, kind="ExternalInput").ap()
    pb_d = nc.dram_tensor("pb", [1, NPB], F32, kind="ExternalInput").ap()
    y_d = nc.dram_tensor("y", [L, D], F32, kind="ExternalOutput").ap()
    dbg_d = {}

    P = Prog()
    B = P.buf
    E = P.emit
    tiles = {}
    stack = []

    def sb(name, shape, dt):
        g = nc.sbuf_tensor(name, list(shape), dt)
        t = g.__enter__()
        stack.append(g)
        tiles[name] = t
        return t

    pstack = []

    def pst(name, shape, dt):
        g = nc.psum_tensor(name, list(shape), dt)
        t = g.__enter__()
        pstack.append(g)
        return t

    MIXT = sb("MIXT", [128, 8, L], BF16)
    XIN = [sb(f"XIN{i}", [128, 1024], F32) for i in range(2)]
    F32T = [sb(f"F32T{i}", [128, 1024], F32) for i in range(2)]
    HBF = sb("HBF", [128, 1024], BF16)
    PPT = sb("PPT", [128, NPP], F32)
    PBT = sb("PBT", [128, 1072], F32)
    identF = sb("identF", [128, 128], F32)
    identB = sb("identB", [128, 128], BF16)
    maskB = sb("maskB", [128, 128], BF16)
    triF = sb("triF", [128, 128], F32)
    onesF = sb("onesF", [128, 128], F32)
    SMALL = sb("SMALL", [128, 256], F32)
    n_common = len(stack)
    WIN = sb("WIN", [128, 8, 4096], BF16)
    HTA = sb("HTA", [128, 8, 128], BF16)
    QT2 = [sb(f"QT{i}", [128, 8, 128], BF16) for i in range(2)]
    KT2 = [sb(f"KT{i}", [128, 8, 128], BF16) for i in range(2)]
    VC2 = [sb(f"VC{i}", [128, 1024], BF16) for i in range(2)]
    SG2 = [sb(f"SG{i}", [128, 1024], F32) for i in range(2)]
    SC2 = [sb(f"SC{i}", [128, 6, 8], F32) for i in range(2)]
    KTT = sb("KTT", [128, 8, 128], BF16)
    ATT = sb("ATT", [128, 8, 128], BF16)
    SST = sb("SST", [128, 8, 128], F32)
    SBF = sb("SBF", [128, 8, 128], BF16)
    TTA = [sb(f"TTA{i}", [128, 512], F32) for i in range(5)]
    TTB = [sb(f"TTB{i}", [128, 512], F32) for i in range(3)]
    RES1 = sb("RES1", [128, 512], BF16)
    TKV = [sb(f"TKV{i}", [128, 128], F32) for i in range(2)]
    LBT = sb("LBT", [128, 32], F32)
    MIXBA = sb("MIXBA", [128, 1024], BF16)
    print("sbuf bytes remaining after pass A alloc:", nc.sbuf_bytes_remaining)
    PS = [pst(f"PS{i}", [128, 512], F32) for i in range(7)]
    PSB_ = pst("PSB", [128, 1024], BF16)
    PSB2 = [PSB_, PSB_]
    psrr = [0, 0]

    def pbank(stage):
        if stage == 1:
            i = psrr[0] % 4
            psrr[0] += 1
        else:
            i = 4 + psrr[1] % 3
            psrr[1] += 1
        return PS[i], B("ps", i)

    bPSB2 = [B("psb", 0), B("psb", 0)]
    psrrA = [0, 0]

    def pbankA(long_lived):
        if long_lived:
            i = 2 + psrrA[1] % 2
            psrrA[1] += 1
        else:
            i = psrrA[0] % 2
            psrrA[0] += 1
        return PS[i], B("ps", i)

    def T(name):
        return B("t", name)

    sm_off = [0]

    def small(n):
        o = sm_off[0]
        sm_off[0] += n
        assert sm_off[0] <= 256
        return SMALL[:, o:o + n], B("small", o)

    def zipper(U, V, u_lo=0.0, u_hi=1.0):
        items = [(u_lo + (u_hi - u_lo) * (i + 0.5) / len(U), 0, i, u) for i, u in enumerate(U)] + [((i + 0.5) / len(V), 1, i, v) for i, v in enumerate(V)]
        items.sort(key=lambda x: (x[0], x[1], x[2]))
        for _, _, _, f in items:
            f()

    def dump(name, ap, shape, buf, dt=F32):
        if name not in dbg:
            return
        d = nc.dram_tensor("dbg_" + name, list(shape), dt, kind="ExternalOutput").ap()
        dbg_d[name] = d
        P.dma(lambda e, d=d, ap=ap: e.dma_start(out=d, in_=ap), "dbg_" + name, reads=[buf])

    semstack = []
    sems = {}
    for k in ["pe", "act", "dve", "pool", "c0", "c1", "stg0", "stg1", "stg2", "xin0", "xin1", "yout0", "yout1"] + ["dbg_" + n for n in dbg]:
        g = nc.semaphore("s_" + k)
        sems[k] = g.__enter__()
        semstack.append(g)

    def replay_all(final=False):
        def mk(eng_name):
            def run(e):
                ops = P.ops[eng_name]
                for waits, fn, inc, _r, _w in ops[P.cursor[eng_name]:]:
                    for k, v in waits:
                        e.wait_ge(sems[k], v)
                    ins = getattr(e, fn[0])(*fn[1], **fn[2])
                    if inc is not None and inc[0] != "sp":
                        ins.then_inc(sems[inc[0]], inc[1])
                P.cursor[eng_name] = len(ops)
                if final and eng_name == "sp":
                    for k, v in P.dmacnt.items():
                        if k.startswith("yout") or k.startswith("dbg_"):
                            e.wait_ge(sems[k], v)
            return run
        with nc.Block() as block:
            block.sync(mk("sp"))
            block.tensor(mk("pe"))
            block.scalar(mk("act"))
            block.vector(mk("dve"))
            block.gpsimd(mk("pool"))

    P.dma(lambda e: e.dma_start(out=PPT[:], in_=pp_d), "c0", writes=[T("PPT")])
    P.dma(lambda e: e.dma_start(out=PBT[:], in_=pb_d[0:1, 0:1072].partition_broadcast(128)), "c1", writes=[T("PBT")])
    E("pool", lambda e: e.memset(identF[:], 0.0), writes=[T("identF")])
    E("pool", lambda e: e.affine_select(out=identF[:], in_=identF[:], pattern=[[-1, 128]], compare_op=ALU.not_equal,
                                        fill=1.0, base=0, channel_multiplier=1), reads=[T("identF")], writes=[T("identF")])
    E("pool", lambda e: e.tensor_copy(out=identB[:], in_=identF[:]), reads=[T("identF")], writes=[T("identB")])
    E("pool", lambda e: e.memset(onesF[:], 1.0), writes=[T("onesF")])
    E("pool", lambda e: e.affine_select(out=triF[:], in_=onesF[:], pattern=[[1, 128]], compare_op=ALU.is_ge,
                                        fill=0.0, base=0, channel_multiplier=-1), reads=[T("onesF")], writes=[T("triF")])
    E("pool", lambda e: e.tensor_copy(out=maskB[:], in_=triF[:]), reads=[T("triF")], writes=[T("maskB")])
    E("pool", lambda e: e.memset(SST[:], 0.0), writes=[T("SST")])
    E("pool", lambda e: e.memset(RES1[:], 1.0), writes=[T("RES1")])
    E("pool", lambda e: e.memset(RES1[:, :].rearrange("p (h t) -> p h t", h=4)[:, :, 0:1], 0.0), reads=[T("RES1")], writes=[T("RES1")])
    E("dve", lambda e: e.tensor_sub(out=LBT[:, 24:32], in0=PPT[:, 24:32], in1=PPT[:, 32:40]), reads=[T("PPT")], writes=[T("LBT")])
    E("act", lambda e: e.activation(out=LBT[:, 0:8], in_=LBT[:, 24:32], func=AF.Exp, scale=-1.0), reads=[T("LBT")], writes=[T("LBT")])
    E("act", lambda e: e.activation(out=LBT[:, 0:8], in_=LBT[:, 0:8], func=AF.Ln, bias=1.0), reads=[T("LBT")], writes=[T("LBT")])
    E("act", lambda e: e.activation(out=LBT[:, 0:8], in_=LBT[:, 0:8], func=AF.Exp, scale=-1.0), reads=[T("LBT")], writes=[T("LBT")])
    E("dve", lambda e: e.tensor_scalar(out=LBT[:, 8:16], in0=LBT[:, 0:8], scalar1=-1.0, scalar2=1.0, op0=ALU.mult, op1=ALU.add),
      reads=[T("LBT")], writes=[T("LBT")])
    E("act", lambda e: e.activation(out=LBT[:, 16:24], in_=LBT[:, 8:16], func=AF.Ln), reads=[T("LBT")], writes=[T("LBT")])

    stg_rr = [0]
    stg_slots = [(F32T[0], T("F32T0"), "stg0"), (F32T[1], T("F32T1"), "stg1"), (XIN[0], T("XIN0"), "xin0"), (XIN[1], T("XIN1"), "xin1")]

    def cast_block(k_eng, dst_ap, st, ncols, scale_ap, sbuf_, dst_buf):
        if k_eng == "act":
            E("act", lambda e: e.activation(out=dst_ap, in_=st[:, 0:ncols], func=AF.Copy, scale=scale_ap),
              reads=[sbuf_, T("PPT")], writes=[dst_buf])
        else:
            E(k_eng, lambda e: e.tensor_scalar(out=dst_ap, in0=st[:, 0:ncols], scalar1=scale_ap, scalar2=1.0, op0=ALU.mult, op1=ALU.mult),
              reads=[sbuf_, T("PPT")], writes=[dst_buf])

    cast_rr = [0]

    def load_w_in(col0, ncols_total, tag):
        wb = T("WIN")
        for k in range(8):
            c = 0
            while c < ncols_total:
                n = min(1024, ncols_total - c)
                st, sbuf_, skey = stg_slots[stg_rr[0] % 4]
                stg_rr[0] += 1
                P.dma(lambda e, st=st, k=k, c=c, n=n: e.dma_start(out=st[:, 0:n], in_=win_d[k * 128:(k + 1) * 128, col0 + c:col0 + c + n]),
                      skey, writes=[sbuf_])
                eng = ("act", "dve")[cast_rr[0] % 2]
                cast_rr[0] += 1
                cast_block(eng, WIN[:, k, c:c + n], st, n, PPT[:, k:k + 1], sbuf_, wb)
                c += n

    def load_w_out():
        wb = T("WOUT")
        for k in range(16):
            st, sbuf_, skey = stg_slots[stg_rr[0] % 4]
            stg_rr[0] += 1
            P.dma(lambda e, st=st, k=k: e.dma_start(out=st[:, :], in_=wout_d[k * 128:(k + 1) * 128, :]), skey, writes=[sbuf_])
            eng = ("act", "dve")[cast_rr[0] % 2]
            cast_rr[0] += 1
            cast_block(eng, WOUT[:, k, :], st, 1024, PPT[:, 8 + k:9 + k], sbuf_, wb)

    def rstd_from_ssq(ssq_ap, ssq_buf, n, out_ap, out_buf):
        E("dve", lambda e: e.tensor_scalar(out=out_ap, in0=ssq_ap, scalar1=1.0 / n, scalar2=EPS, op0=ALU.mult, op1=ALU.add),
          reads=[ssq_buf], writes=[out_buf])
        E("act", lambda e: e.activation(out=out_ap, in_=out_ap, func=AF.Ln), reads=[out_buf], writes=[out_buf])
        E("act", lambda e: e.activation(out=out_ap, in_=out_ap, func=AF.Exp, scale=-0.5), reads=[out_buf], writes=[out_buf])

    ssq1, b_ssq1 = small(1)
    rs1, b_rs1 = small(1)

    def x_load(c):
        sl = c % 2
        P.dma(lambda e: e.dma_start(out=XIN[sl][:], in_=x_d[c * 128:(c + 1) * 128, :]), f"xin{sl}", writes=[T(f"XIN{sl}")])

    def x_norm_T(c, HT, bHT):
        sl = c % 2
        xb = T(f"XIN{sl}")
        E("act", lambda e: e.activation(out=HBF[:], in_=XIN[sl][:], func=AF.Square, accum_out=ssq1), reads=[xb], writes=[T("HBF"), b_ssq1])
        rstd_from_ssq(ssq1, b_ssq1, 1024.0, rs1, b_rs1)
        E("act", lambda e: e.activation(out=HBF[:], in_=XIN[sl][:], func=AF.Copy, scale=rs1), reads=[xb, b_rs1], writes=[T("HBF")])
        for k in range(8):
            E("pe", lambda e, k=k: e.transpose(out=PSB2[0][:, k * 128:(k + 1) * 128], in_=HBF[:, k * 128:(k + 1) * 128], identity=identB[:]),
              reads=[T("HBF"), T("identB")], writes=[bPSB2[0]])
        E("dve", lambda e: e.tensor_copy(out=HT[:, :, :], in_=PSB2[0][:, :].rearrange("p (k t) -> p k t", k=8)), reads=[bPSB2[0]], writes=[bHT])

    def silu_from_psum(dst, dbuf, src, sbuf_, tmp=None, tbuf=None):
        t_, tb_ = (dst, dbuf) if tmp is None else (tmp, tbuf)
        E("act", lambda e: e.activation(out=t_, in_=src, func=AF.Exp, scale=-1.0), reads=[sbuf_], writes=[tb_])
        E("act", lambda e: e.activation(out=t_, in_=t_, func=AF.Ln, bias=1.0), reads=[tb_], writes=[tb_])
        E("act", lambda e: e.activation(out=t_, in_=t_, func=AF.Exp, scale=-1.0), reads=[tb_], writes=[tb_])
        E("dve", lambda e: e.tensor_tensor(out=dst, in0=src, in1=t_, op=ALU.mult), reads=[sbuf_, tb_], writes=[dbuf])

    def proj_tm(HT, bHT, col0, ncols, ps, psbuf, pcol0=0):
        for k in range(8):
            E("pe", lambda e, k=k: e.matmul(ps[:, pcol0:pcol0 + ncols], lhsT=HT[:, k, :], rhs=WIN[:, k, col0:col0 + ncols],
                                            start=(k == 0), stop=(k == 7)),
              reads=[bHT, T("WIN")], writes=[psbuf])

    def proj_fm(HT, bHT, col0, ps, psbuf, pcol0=0):
        for k in range(8):
            E("pe", lambda e, k=k: e.matmul(ps[:, pcol0:pcol0 + 128], lhsT=WIN[:, k, col0:col0 + 128], rhs=HT[:, k, :],
                                            start=(k == 0), stop=(k == 7)),
              reads=[bHT, T("WIN")], writes=[psbuf])

    load_w_in(0, 4096, "A")
    ssq8, b_ssq8 = small(8)
    rs8, b_rs8 = small(8)

    def A_xn(c):
        if c + 1 < NCH:
            x_load(c + 1)
        x_norm_T(c, HTA, T("HTA"))

    def A_S1(c):
        p = c % 2
        QT, KT, VC, SG, SC = QT2[p], KT2[p], VC2[p], SG2[p], SC2[p]
        bVC, bSG = B("VC", p), B("SG", p)
        bHT = T("HTA")
        U = []

        def u_v(half):
            ps, pb_ = pbankA(False)
            proj_tm(HTA, bHT, 2048 + half * 512, 512, ps, pb_)
            E("dve", lambda e: e.tensor_copy(out=VC[:, half * 512:(half + 1) * 512], in_=ps[:, :]), reads=[pb_], writes=[bVC])

        def u_g(half):
            ps, pb_ = pbankA(False)
            proj_tm(HTA, bHT, 3072 + half * 512, 512, ps, pb_)
            silu_from_psum(SG[:, half * 512:(half + 1) * 512], bSG, ps[:, :], pb_)
        for half in range(2):
            U.append(lambda half=half: u_v(half))
        for half in range(2):
            U.append(lambda half=half: u_g(half))

        def grp(g):
            hs = g * 4
            if g == 0:
                te, tsp, tl, tb, tq = TTA
                be, bsp, bl, bb, bq = [T(f"TTA{i}") for i in range(5)]
            else:
                te, tsp, tl = TTB
                be, bsp, bl = [T(f"TTB{i}") for i in range(3)]
                tb, tq = F32T[0][:, 0:512], F32T[0][:, 512:1024]
                bb, bq = B("F0a"), B("F0b")
                te, tsp, tl = te[:, :], tsp[:, :], tl[:, :]
            if g == 0:
                te, tsp, tl, tb, tq = te[:, :], tsp[:, :], tl[:, :], tb[:, :], tq[:, :]
            bsc = B("SC", p, g)
            st = {}
            v4 = lambda ap: ap.rearrange("p (h t) -> p h t", h=4)

            def uA():
                psq, pbq = pbankA(True)
                psf, pbf = pbankA(False)
                st["psq"], st["pbq"] = psq, pbq
                for hh in range(4):
                    proj_fm(HTA, bHT, (hs + hh) * 128, psq, pbq, hh * 128)
                for hh in range(4):
                    proj_fm(HTA, bHT, 1024 + (hs + hh) * 128, psf, pbf, hh * 128)
                E("act", lambda e: e.activation(out=te, in_=psf[:, :], func=AF.Exp, scale=-1.0), reads=[pbf], writes=[be])
                E("act", lambda e: e.activation(out=tq, in_=psq[:, :], func=AF.Exp, scale=-1.0), reads=[pbq], writes=[bq])
                E("act", lambda e: e.activation(out=tsp, in_=te, func=AF.Ln, bias=1.0), reads=[be], writes=[bsp])
                E("act", lambda e: e.activation(out=tq, in_=tq, func=AF.Ln, bias=1.0), reads=[bq], writes=[bq])
                for hh in range(4):
                    h = hs + hh
                    E("act", lambda e, hh=hh, h=h: e.activation(out=tl[:, hh * 128:(hh + 1) * 128], in_=te[:, hh * 128:(hh + 1) * 128], func=AF.Ln,
                                                                scale=LBT[:, h:h + 1], bias=1.0), reads=[be, T("LBT")], writes=[bl])

            def uB():
                E("dve", lambda e: e.tensor_sub(out=tl, in0=tl, in1=tsp), reads=[bsp, bl], writes=[bl])
                E("dve", lambda e: e.tensor_tensor_scan(out=tb, data0=RES1[:, :], data1=tl, initial=0.0, op0=ALU.mult, op1=ALU.add),
                  reads=[bl, T("RES1")], writes=[bb])
                b63 = v4(tb)[:, :, 63]
                b127 = v4(tb)[:, :, 127]
                E("dve", lambda e: e.tensor_scalar(out=SC[:, 0, hs:hs + 4], in0=b63, scalar1=-1.0, scalar2=1.0, op0=ALU.mult, op1=ALU.mult), reads=[bb], writes=[bsc])
                E("dve", lambda e: e.tensor_sub(out=SC[:, 4, hs:hs + 4], in0=b127, in1=b63), reads=[bb, bsc], writes=[bsc])
                E("dve", lambda e: e.tensor_add(out=SC[:, 5, hs:hs + 4], in0=b63, in1=LBT[:, 16 + hs:20 + hs]), reads=[bb, bsc, T("LBT")], writes=[bsc])
                E("act", lambda e: e.activation(out=SC[:, 1, hs:hs + 4], in_=b63, func=AF.Exp), reads=[bb, bsc], writes=[bsc])
                E("act", lambda e: e.activation(out=SC[:, 2, hs:hs + 4], in_=b127, func=AF.Exp), reads=[bb, bsc], writes=[bsc])
                E("act", lambda e: e.activation(out=SC[:, 3, hs:hs + 4], in_=SC[:, 4, hs:hs + 4], func=AF.Exp), reads=[bsc], writes=[bsc])
                E("dve", lambda e: e.tensor_add(out=tsp, in0=tsp, in1=tb), reads=[bsp, bb], writes=[bsp])
                E("dve", lambda e: e.tensor_sub(out=tq, in0=tb, in1=tq), reads=[bq, bb], writes=[bq])

            def uC():
                psq, pbq = st["psq"], st["pbq"]
                for hh in range(4):
                    h = hs + hh
                    E("act", lambda e, hh=hh, h=h: e.activation(out=tsp[:, hh * 128:(hh + 1) * 128], in_=tsp[:, hh * 128:(hh + 1) * 128], func=AF.Exp, scale=-1.0,
                                                                bias=SC[:, 5, h:h + 1]), reads=[bsp, bsc], writes=[bsp])
                for hh in range(4):
                    h = hs + hh
                    E("act", lambda e, hh=hh, h=h: e.activation(out=tq[:, hh * 128:(hh + 1) * 128], in_=tq[:, hh * 128:(hh + 1) * 128], func=AF.Exp,
                                                                bias=SC[:, 0, h:h + 1]), reads=[bq, bsc], writes=[bq])
                E("pool", lambda e: e.tensor_tensor(out=KT[:, hs:hs + 4, :], in0=v4(te), in1=v4(tsp), op=ALU.mult), reads=[be, bsp], writes=[B("KT", p, g)])
                E("dve", lambda e: e.tensor_tensor(out=QT[:, hs:hs + 4, :], in0=v4(psq[:, :]), in1=v4(tq), op=ALU.mult), reads=[pbq, bq], writes=[B("QT", p, g)])
            return uA, uB, uC
        a0_, b0_, c0_ = grp(0)
        a1_, b1_, c1_ = grp(1)
        U += [a0_, a1_, b0_, b1_]
        if c + 1 < NCH:
            U.append(lambda: A_xn(c + 1))
        U += [c0_, c1_]
        return U

    def A_S2(c):
        p = c % 2
        QT, KT, VC, SG, SC = QT2[p], KT2[p], VC2[p], SG2[p], SC2[p]
        bVC, bSG = B("VC", p), B("SG", p)
        O32 = F32T[1]
        bO = T("F32T1")
        V = []

        def v_att(hb):
            ps, pb_ = pbank(2)
            for hh in range(4):
                h = hb * 4 + hh
                E("pe", lambda e, hh=hh, h=h: e.matmul(ps[:, hh * 128:(hh + 1) * 128], lhsT=KT[:, h, :], rhs=QT[:, h, :], start=True, stop=True),
                  reads=[B("KT", p, h // 4), B("QT", p, h // 4)], writes=[pb_])
            E("dve", lambda e: e.tensor_tensor(out=ATT[:, hb * 4:(hb + 1) * 4, :], in0=ps[:, :].rearrange("p (h t) -> p h t", h=4),
                                               in1=maskB[:, :].unsqueeze(1).to_broadcast([128, 4, 128]), op=ALU.mult),
              reads=[pb_, T("maskB")], writes=[B("ATT", hb)])

        def v_ktt():
            for h in range(8):
                E("pe", lambda e, h=h: e.transpose(out=PSB2[1][:, h * 128:(h + 1) * 128], in_=KT[:, h, :], identity=identB[:]),
                  reads=[B("KT", p, h // 4), T("identB")], writes=[bPSB2[1]])
            E("act", lambda e: e.activation(out=KTT[:, :, :], in_=PSB2[1][:, :].rearrange("p (k t) -> p k t", k=8), func=AF.Copy), reads=[bPSB2[1]], writes=[T("KTT")])

        def v_sbf():
            for h in range(8):
                E("dve", lambda e, h=h: e.tensor_scalar(out=SBF[:, h, :], in0=SST[:, h, :], scalar1=SC[:, 1, h:h + 1], scalar2=1.0, op0=ALU.mult, op1=ALU.mult),
                  reads=[T("SST"), B("SC", p, h // 4)], writes=[B("SBF", h)])

        def v_o(hb):
            ps, pb_ = pbank(2)
            for hh in range(4):
                h = hb * 4 + hh
                E("pe", lambda e, hh=hh, h=h: e.matmul(ps[:, hh * 128:(hh + 1) * 128], lhsT=ATT[:, h, :], rhs=VC[:, h * 128:(h + 1) * 128], start=True, stop=False),
                  reads=[B("ATT", hb), bVC], writes=[pb_])
                E("pe", lambda e, hh=hh, h=h: e.matmul(ps[:, hh * 128:(hh + 1) * 128], lhsT=QT[:, h, :], rhs=SBF[:, h, :], start=False, stop=True),
                  reads=[B("QT", p, h // 4), B("SBF", h)], writes=[pb_])
            E("act", lambda e: e.activation(out=O32[:, hb * 512:(hb + 1) * 512], in_=ps[:, :], func=AF.Copy), reads=[pb_], writes=[bO])

        def v_kv(hb):
            ps, pb_ = pbank(2)
            for hh in range(4):
                h = hb * 4 + hh
                E("pe", lambda e, hh=hh, h=h: e.matmul(ps[:, hh * 128:(hh + 1) * 128], lhsT=KTT[:, h, :], rhs=VC[:, h * 128:(h + 1) * 128], start=True, stop=True),
                  reads=[T("KTT"), bVC], writes=[pb_])
            for hh in range(4):
                h = hb * 4 + hh
                tk = TKV[h % 2]
                btk = T(f"TKV{h % 2}")
                E("dve", lambda e, hh=hh, h=h, tk=tk: e.tensor_scalar(out=tk[:], in0=ps[:, hh * 128:(hh + 1) * 128], scalar1=SC[:, 3, h:h + 1], scalar2=1.0,
                                                                      op0=ALU.mult, op1=ALU.mult),
                  reads=[pb_, B("SC", p, h // 4)], writes=[btk])
                E("dve", lambda e, h=h, tk=tk: e.scalar_tensor_tensor(out=SST[:, h, :], in0=SST[:, h, :], scalar=SC[:, 2, h:h + 1], in1=tk[:], op0=ALU.mult, op1=ALU.add),
                  reads=[btk, B("SC", p, h // 4), T("SST"), B("SBF", h)], writes=[T("SST")])

        def v_norm():
            E("dve", lambda e: e.tensor_mul(out=MIXBA[:], in0=O32[:], in1=O32[:]), reads=[bO], writes=[T("MIXBA")])
            E("dve", lambda e: e.tensor_reduce(out=ssq8, in_=MIXBA[:, :].rearrange("p (h v) -> p h v", h=8), axis=AX.X, op=ALU.add), reads=[T("MIXBA")], writes=[b_ssq8])
            rstd_from_ssq(ssq8, b_ssq8, 128.0, rs8, b_rs8)
            E("dve", lambda e: e.tensor_tensor(out=O32[:, :].rearrange("p (h v) -> p h v", h=8), in0=O32[:, :].rearrange("p (h v) -> p h v", h=8),
                                               in1=rs8.unsqueeze(2).to_broadcast([128, 8, 128]), op=ALU.mult), reads=[bO, b_rs8], writes=[bO])
            E("dve", lambda e: e.tensor_mul(out=MIXBA[:], in0=O32[:], in1=SG[:]), reads=[bO, bSG], writes=[T("MIXBA")])

        def v_mixt():
            for k in range(8):
                E("pe", lambda e, k=k: e.transpose(out=PSB2[1][:, k * 128:(k + 1) * 128], in_=MIXBA[:, k * 128:(k + 1) * 128], identity=identB[:]),
                  reads=[T("MIXBA"), T("identB")], writes=[bPSB2[1]])
            E("act", lambda e: e.activation(out=MIXT[:, :, c * 128:(c + 1) * 128], in_=PSB2[1][:, :].rearrange("p (k t) -> p k t", k=8), func=AF.Copy),
              reads=[bPSB2[1]], writes=[T("MIXT")])

        V += [lambda: v_att(0), lambda: v_att(1), v_ktt, v_sbf, lambda: v_o(0), lambda: v_o(1), lambda: v_kv(0), lambda: v_kv(1), v_norm, v_mixt]
        return V

    x_load(0)
    A_xn(0)
    for u in A_S1(0):
        u()
    for c in range(NCH):
        U = A_S1(c + 1) if c + 1 < NCH else []
        V = A_S2(c)
        if U:
            zipper(U, V)
        else:
            for v in V:
                v()

    replay_all()
    while len(stack) > n_common:
        stack.pop().__exit__(None, None, None)
    WIN = sb("WINB", [128, 8, 2576], BF16)
    WOUT = sb("WOUT", [128, 16, 1024], BF16)
    HTB = [sb(f"HTB{i}", [128, 8, 128], BF16) for i in range(2)]
    XBCT = sb("XBCT", [128, 12, 131], BF16)
    XSB2 = [sb(f"XSB{i}", [128, 1024], BF16) for i in range(2)]
    BTM2 = [sb(f"BTM{i}", [128, 256], BF16) for i in range(2)]
    BCT2 = [sb(f"BCT{i}", [128, 4, 128], BF16) for i in range(2)]
    CS2T2 = [sb(f"CS2T{i}", [48, 128], BF16) for i in range(2)]
    NCS2T2 = [sb(f"NCS2T{i}", [48, 128], BF16) for i in range(2)]
    XDT = sb("XDT", [128, 1024], BF16)
    XDD = sb("XDD", [128, 1024], BF16)
    XSD = sb("XSD", [128, 1024], BF16)
    LT = sb("LT", [128, 16, 128], BF16)
    CBM = sb("CBM", [128, 2, 128], BF16)
    ST32 = sb("ST32", [128, 1024], F32)
    STB = sb("STB", [128, 1024], BF16)
    MIXT2 = sb("MIXT2", [128, 8, 128], BF16)
    DG = sb("DG", [128, 8, 128], BF16)
    SEL = sb("SEL", [48, 16, 128], BF16)
    HI48 = sb("HI48", [48, 128], BF16)
    APADH = sb("APADH", [128, 48], BF16)
    APADL = sb("APADL", [128, 48], BF16)
    onesB = sb("onesB", [128, 128], BF16)
    NEGM = sb("NEGM", [128, 128], BF16)
    CBROW = sb("CBROW", [1, 1280], BF16)
    ONESROW = sb("ONESROW", [1, 128], BF16)
    print("sbuf bytes remaining before TMPB:", nc.sbuf_bytes_remaining)
    TMPB = sb("TMPB", [128, 2, 256], F32)
    print("sbuf bytes remaining after pass B alloc:", nc.sbuf_bytes_remaining)
    tmpb_rr = [0]

    def tmpb(n):
        i = tmpb_rr[0] % 2
        tmpb_rr[0] += 1
        return TMPB[:, i, 0:n], B("TMPB", i)
    P.barrier()
    E("pool", lambda e: e.memset(ST32[:], 0.0), writes=[T("ST32")])
    E("pool", lambda e: e.memset(STB[:], 0.0), writes=[T("STB")])
    E("pool", lambda e: e.memset(XBCT[:], 0.0), writes=[T("XBCT")])
    for i in range(2):
        E("pool", lambda e, i=i: e.memset(CS2T2[i][:], 0.0), writes=[B("CS2T", i)])
    E("pool", lambda e: e.memset(ONESROW[:], 1.0), writes=[T("ONESROW")])
    E("pool", lambda e: e.memset(APADH[:], 0.0), writes=[T("APADH")])
    E("pool", lambda e: e.memset(APADL[:], 0.0), writes=[T("APADL")])
    E("pool", lambda e: e.memset(onesB[:], 1.0), writes=[T("onesB")])
    E("pool", lambda e: e.memset(SEL[:], 0.0), writes=[T("SEL")])
    for base in (0, 32):
        E("pool", lambda e, base=base: e.affine_select(
            out=SEL[base:base + 16, :, :], in_=SEL[base:base + 16, :, :], pattern=[[-1, 16], [0, 128]],
            compare_op=ALU.not_equal, fill=1.0, base=0, channel_multiplier=1), reads=[T("SEL")], writes=[T("SEL")])
    E("pool", lambda e: e.tensor_scalar(out=NEGM[:, :], in0=triF[:], scalar1=-1.0, scalar2=30000.0, op0=ALU.add, op1=ALU.mult),
      reads=[T("triF")], writes=[T("NEGM")])
    load_w_in(4096, 2576, "B")
    load_w_out()
    P.dma(lambda e: e.dma_start(out=F32T[0][0:1, 0:1024], in_=pb_d[0:1, 1072:2096]), "stg0", writes=[T("F32T0")])
    P.dma(lambda e: e.dma_start(out=F32T[1][0:1, 0:256], in_=pb_d[0:1, 2096:2352]), "stg1", writes=[T("F32T1")])
    E("dve", lambda e: e.tensor_copy(out=CBROW[0:1, 0:1024], in_=F32T[0][0:1, 0:1024]), reads=[T("F32T0")], writes=[T("CBROW")])
    E("dve", lambda e: e.tensor_copy(out=CBROW[0:1, 1024:1280], in_=F32T[1][0:1, 0:256]), reads=[T("F32T1")], writes=[T("CBROW")])
    Abc, bA = small(16)
    E("act", lambda e: e.activation(out=Abc, in_=PBT[:, 16:32], func=AF.Exp), reads=[T("PBT")], writes=[bA])
    E("dve", lambda e: e.tensor_scalar(out=Abc, in0=Abc, scalar1=-1.0, scalar2=1.0, op0=ALU.mult, op1=ALU.mult), reads=[bA], writes=[bA])
    ncb, b_ncb = small(12)
    E("dve", lambda e: e.tensor_scalar(out=ncb, in0=PPT[:, 88:100], scalar1=-1.0, scalar2=1.0, op0=ALU.mult, op1=ALU.mult), reads=[T("PPT")], writes=[b_ncb])
    av, b_a = small(16)
    cssb, b_cs = small(32)
    dtv2 = [small(16) for _ in range(2)]
    ecs2 = [small(16) for _ in range(2)]
    ddv2 = [small(16) for _ in range(2)]
    etot2 = [small(16) for _ in range(2)]
    ssq2, b_ssq2 = small(2)
    rs2, b_rs2 = small(2)
    ssqf, b_ssqf = small(1)
    rsf, b_rsf = small(1)
    dg_rr = [0]

    def B_xn(c):
        if c + 1 < NCH:
            x_load(c + 1)
        x_norm_T(c, HTB[c % 2], B("HTB", c % 2))

    def B_S1(c):
        p = c % 2
        HT, bHT = HTB[p], B("HTB", p)
        XSB, BTM, BCT, CS2T, NCS2T = XSB2[p], BTM2[p], BCT2[p], CS2T2[p], NCS2T2[p]
        dtv, b_dt = dtv2[p]
        ecs, b_ecs = ecs2[p]
        ddv, b_dd = ddv2[p]
        etot, b_etot = etot2[p]
        U = []

        def u_first():
            gen_dg(jorder[0])
            gen_dg(jorder[1])
            B_xn(c)
        U.append(u_first)

        xb_st = {}

        def u_xbc(jb, jj):
            if jj == 0:
                xb_st[jb] = pbank(1)
            ps, pb_ = xb_st[jb]
            proj_fm(HT, bHT, 1024 + (jb * 4 + jj) * 128, ps, pb_, jj * 128)
            if jj == 3:
                E("dve", lambda e: e.tensor_copy(out=XBCT[:, jb * 4:(jb + 1) * 4, 3:131], in_=ps[:, :].rearrange("p (j t) -> p j t", j=4)),
                  reads=[pb_], writes=[T("XBCT")])
        for jb in range(3):
            for jj in range(4):
                U.append(lambda jb=jb, jj=jj: u_xbc(jb, jj))

        def u_dt():
            ps_dt, pb_dt = pbank(1)
            proj_tm(HT, bHT, 2560, 16, ps_dt, pb_dt)
            E("dve", lambda e: e.tensor_tensor(out=dtv, in0=ps_dt[:, 0:16], in1=PBT[:, 0:16], op=ALU.add), reads=[pb_dt, T("PBT")], writes=[b_dt])
            E("act", lambda e: e.activation(out=dtv, in_=dtv, func=AF.Exp), reads=[b_dt], writes=[b_dt])
            E("act", lambda e: e.activation(out=dtv, in_=dtv, func=AF.Ln, bias=1.0), reads=[b_dt], writes=[b_dt])
            E("dve", lambda e: e.tensor_mul(out=av, in0=dtv, in1=Abc), reads=[b_dt, bA], writes=[b_a])
            E("dve", lambda e: e.tensor_copy(out=APADH[:, 0:16], in_=av), reads=[b_a, T("APADH")], writes=[T("APADH")])
            E("dve", lambda e: e.tensor_sub(out=APADL[:, 0:16], in0=av, in1=APADH[:, 0:16]), reads=[b_a, T("APADH"), T("APADL")], writes=[T("APADL")])
            E("dve", lambda e: e.tensor_copy(out=APADH[:, 32:48], in_=APADH[:, 0:16]), reads=[T("APADH")], writes=[T("APADH")])
            E("dve", lambda e: e.tensor_copy(out=APADL[:, 32:48], in_=APADL[:, 0:16]), reads=[T("APADL")], writes=[T("APADL")])
            ps_s, pb_s = pbank(1)
            for i_, ap_ in enumerate((APADH, APADL)):
                E("pe", lambda e, i_=i_, ap_=ap_: e.matmul(ps_s[:, 0:16], lhsT=maskB[:], rhs=ap_[:, 0:16], start=(i_ == 0), stop=(i_ == 1)),
                  reads=[T("maskB"), T("APADH"), T("APADL")], writes=[pb_s])
            for i_, ap_ in enumerate((APADH, APADL)):
                E("pe", lambda e, i_=i_, ap_=ap_: e.matmul(ps_s[:, 16:32], lhsT=onesB[:], rhs=ap_[:, 0:16], start=(i_ == 0), stop=(i_ == 1)),
                  reads=[T("onesB"), T("APADH"), T("APADL")], writes=[pb_s])
            for i_, ap_ in enumerate((APADH, APADL)):
                E("pe", lambda e, i_=i_, ap_=ap_: e.matmul(ps_s[0:48, 128:256], lhsT=ap_[:, 0:48], rhs=maskB[:], start=(i_ == 0), stop=(i_ == 1)),
                  reads=[T("maskB"), T("APADH"), T("APADL")], writes=[pb_s])
            E("dve", lambda e: e.tensor_copy(out=cssb, in_=ps_s[:, 0:32]), reads=[pb_s], writes=[b_cs])
            E("act", lambda e: e.activation(out=ecs, in_=cssb[:, 0:16], func=AF.Exp), reads=[b_cs], writes=[b_ecs])
            E("act", lambda e: e.activation(out=etot, in_=cssb[:, 16:32], func=AF.Exp), reads=[b_cs], writes=[b_etot])
            E("dve", lambda e: e.tensor_sub(out=ddv, in0=cssb[:, 16:32], in1=cssb[:, 0:16]), reads=[b_cs], writes=[b_dd])
            E("act", lambda e: e.activation(out=ddv, in_=ddv, func=AF.Exp), reads=[b_dd], writes=[b_dd])
            E("dve", lambda e: e.tensor_mul(out=ddv, in0=ddv, in1=dtv), reads=[b_dd, b_dt], writes=[b_dd])
            E("dve", lambda e: e.tensor_copy(out=HI48[:, :], in_=ps_s[0:48, 128:256]), reads=[pb_s], writes=[T("HI48")])
            E("dve", lambda e: e.tensor_copy(out=CS2T[0:16, :], in_=HI48[0:16, :]), reads=[T("HI48")], writes=[B("CS2T", p)])
            E("dve", lambda e: e.tensor_sub(out=CS2T[32:48, :], in0=ps_s[32:48, 128:256], in1=HI48[32:48, :]), reads=[pb_s, T("HI48"), B("CS2T", p)], writes=[B("CS2T", p)])
            E("dve", lambda e: e.tensor_scalar(out=NCS2T[:, :], in0=CS2T[:, :], scalar1=-1.0, scalar2=1.0, op0=ALU.mult, op1=ALU.mult), reads=[B("CS2T", p)], writes=[B("NCS2T", p)])
        U.append(u_dt)

        conv = {"dg": {}}
        jorder = (8, 0, 1, 9, 2, 3, 10, 4, 5, 11, 6, 7)

        def gen_dg(j):
            dgs = []
            for tap in range(4):
                sl = dg_rr[0] % 8
                dg_rr[0] += 1
                bd = B("DG", sl)
                wcol = PPT[:, 40 + j * 4 + tap:41 + j * 4 + tap]
                E("pool", lambda e, sl=sl, wcol=wcol: e.tensor_scalar(out=DG[:, sl, :], in0=identB[:], scalar1=wcol, scalar2=1.0, op0=ALU.mult, op1=ALU.mult),
                  reads=[T("identB"), T("PPT")], writes=[bd])
                dgs.append((sl, bd))
            conv["dg"][j] = dgs

        def u_conv(j):
            if "ps_x" not in conv:
                conv["ps_x"] = [pbank(1), pbank(1)]
                conv["ps_b"] = pbank(1)
                conv["ps_f"] = pbank(1)
            dgs = conv["dg"].pop(j)
            for tap in range(0):
                sl = dg_rr[0] % 8
                dg_rr[0] += 1
                bd = B("DG", sl)
                wcol = PPT[:, 40 + j * 4 + tap:41 + j * 4 + tap]
                geng = "pool"
                if geng == "act":
                    E("act", lambda e, sl=sl, wcol=wcol: e.activation(out=DG[:, sl, :], in_=identB[:], func=AF.Copy, scale=wcol), reads=[T("identB"), T("PPT")], writes=[bd])
                else:
                    E(geng, lambda e, sl=sl, wcol=wcol: e.tensor_scalar(out=DG[:, sl, :], in0=identB[:], scalar1=wcol, scalar2=1.0, op0=ALU.mult, op1=ALU.mult),
                      reads=[T("identB"), T("PPT")], writes=[bd])
                dgs.append((sl, bd))
            if j < 10:
                if j < 8:
                    ps, pb_ = conv["ps_x"][j // 4]
                    pc = (j % 4) * 128
                else:
                    ps, pb_ = conv["ps_b"]
                    pc = (j - 8) * 128
                for tap in range(4):
                    sl, bd = dgs[tap]
                    E("pe", lambda e, tap=tap, sl=sl: e.matmul(ps[:, pc:pc + 128], lhsT=XBCT[:, j, tap:tap + 128], rhs=DG[:, sl, :],
                                                               start=(tap == 0), stop=False),
                      reads=[T("XBCT"), bd], writes=[pb_])
                E("pe", lambda e: e.matmul(ps[:, pc:pc + 128], lhsT=ONESROW[0:1, :], rhs=CBROW[0:1, j * 128:(j + 1) * 128], start=False, stop=True),
                  reads=[T("ONESROW"), T("CBROW")], writes=[pb_])
            if j >= 8:
                ps, pb_ = conv["ps_f"]
                pc = (j - 8) * 128
                for tap in range(4):
                    sl, bd = dgs[tap]
                    E("pe", lambda e, tap=tap, sl=sl: e.matmul(ps[:, pc:pc + 128], lhsT=DG[:, sl, :], rhs=XBCT[:, j, tap:tap + 128],
                                                               start=(tap == 0), stop=(tap == 3)),
                      reads=[T("XBCT"), bd], writes=[pb_])
                tb, tbb = tmpb(128)
                E("act", lambda e: e.activation(out=tb, in_=ps[:, pc:pc + 128], func=AF.Exp, scale=-1.0, bias=ncb[:, j:j + 1]),
                  reads=[pb_, b_ncb], writes=[tbb])
                E("act", lambda e: e.activation(out=tb, in_=tb, func=AF.Ln, bias=1.0), reads=[tbb], writes=[tbb])
                E("act", lambda e: e.activation(out=tb, in_=tb, func=AF.Exp, scale=-1.0), reads=[tbb], writes=[tbb])
                E("dve", lambda e: e.scalar_tensor_tensor(out=BCT[:, j - 8, :], in0=ps[:, pc:pc + 128], scalar=PPT[:, 88 + j:89 + j], in1=tb,
                                                          op0=ALU.add, op1=ALU.mult),
                  reads=[pb_, T("PPT"), tbb], writes=[B("BCT", p, j - 8)])
        def u_conv_i(i):
            u_conv(jorder[i])
            if i + 2 < 12:
                gen_dg(jorder[i + 2])
        for i in range(12):
            U.append(lambda i=i: u_conv_i(i))

        def u_xs():
            E("dve", lambda e: e.tensor_copy(out=XBCT[:, :, 0:3], in_=XBCT[:, :, 128:131]), reads=[T("XBCT")], writes=[T("XBCT")])
            for half in range(2):
                ps, pb_ = conv["ps_x"][half]
                for qq in range(2):
                    tb, tbb = tmpb(256)
                    silu_from_psum(XSB[:, half * 512 + qq * 256:half * 512 + (qq + 1) * 256], B("XSB", p), ps[:, qq * 256:(qq + 1) * 256], pb_, tmp=tb, tbuf=tbb)
            tb, tbb = tmpb(256)
            silu_from_psum(BTM[:, :], B("BTM", p), conv["ps_b"][0][:, 0:256], conv["ps_b"][1], tmp=tb, tbuf=tbb)
        U.append(u_xs)
        return U

    def B_S2(c):
        p = c % 2
        HT, bHT = HTB[p], B("HTB", p)
        XSB, BTM, BCT, CS2T, NCS2T = XSB2[p], BTM2[p], BCT2[p], CS2T2[p], NCS2T2[p]
        dtv, b_dt = dtv2[p]
        ecs, b_ecs = ecs2[p]
        ddv, b_dd = ddv2[p]
        etot, b_etot = etot2[p]
        sl = c % 2
        SZ, bSZ = F32T[sl], T(f"F32T{sl}")
        Y32, bY = F32T[1 - sl], T(f"F32T{1 - sl}")
        OUT, bOUT = SZ, bSZ
        V = []

        def v_z(half):
            ps, pb_ = pbank(2)
            proj_tm(HT, bHT, half * 512, 512, ps, pb_)
            silu_from_psum(SZ[:, half * 512:(half + 1) * 512], bSZ, ps[:, :], pb_)
        V += [lambda: v_z(0), lambda: v_z(1)]

        def v_scale():
            v3 = lambda t: t[:, :].rearrange("p (h q) -> p h q", h=16)
            bc3 = lambda v: v.unsqueeze(2).to_broadcast([128, 16, 64])
            E("dve", lambda e: e.tensor_tensor(out=v3(XDT), in0=v3(XSB), in1=bc3(dtv), op=ALU.mult), reads=[B("XSB", p), b_dt], writes=[T("XDT")])
            E("dve", lambda e: e.tensor_tensor(out=v3(XDD), in0=v3(XSB), in1=bc3(ddv), op=ALU.mult), reads=[B("XSB", p), b_dd], writes=[T("XDD")])
            E("dve", lambda e: e.tensor_tensor(out=v3(XSD), in0=v3(XSB), in1=bc3(PBT[:, 32:48]), op=ALU.mult), reads=[B("XSB", p), T("PBT")], writes=[T("XSD")])
        V.append(v_scale)

        def v_cb():
            ps, pb_ = pbank(2)
            for g in range(2):
                E("pe", lambda e, g=g: e.matmul(ps[:, g * 128:(g + 1) * 128], lhsT=BCT[:, g, :], rhs=BCT[:, 2 + g, :], start=True, stop=True),
                  reads=[B("BCT", p, g), B("BCT", p, 2 + g)], writes=[pb_])
            E("dve", lambda e: e.tensor_tensor(out=CBM[:, :, :], in0=ps[:, 0:256].rearrange("p (g t) -> p g t", g=2),
                                               in1=maskB[:, :].unsqueeze(1).to_broadcast([128, 2, 128]), op=ALU.mult), reads=[pb_, T("maskB")], writes=[T("CBM")])
        V.append(v_cb)

        def v_seg(hb):
            ps, pb_ = pbank(2)
            for hh in range(4):
                h = hb * 4 + hh
                E("pe", lambda e, hh=hh: e.matmul(ps[:, hh * 128:(hh + 1) * 128], lhsT=identB[:], rhs=NEGM[:, :], start=True, stop=False),
                  reads=[T("identB"), T("NEGM")], writes=[pb_])
                E("pe", lambda e, hh=hh, h=h: e.matmul(ps[:, hh * 128:(hh + 1) * 128], lhsT=SEL[0:48, h, :], rhs=CS2T[0:48, :], start=False, stop=False),
                  reads=[T("SEL"), B("CS2T", p)], writes=[pb_])
                E("pe", lambda e, hh=hh, h=h: e.matmul(ps[:, hh * 128:(hh + 1) * 128], lhsT=NCS2T[0:48, :], rhs=SEL[0:48, h, :], start=False, stop=True),
                  reads=[T("SEL"), B("NCS2T", p)], writes=[pb_])
            E("act", lambda e: e.activation(out=LT[:, hb * 4:(hb + 1) * 4, :], in_=ps[:, :].rearrange("p (h t) -> p h t", h=4), func=AF.Exp),
              reads=[pb_], writes=[B("LT", hb // 2)])
        for hb in range(4):
            V.append(lambda hb=hb: v_seg(hb))

        def v_y(g):
            E("dve", lambda e: e.scalar_tensor_tensor(out=LT[:, g * 8:(g + 1) * 8, :], in0=LT[:, g * 8:(g + 1) * 8, :], scalar=1.0,
                                                      in1=CBM[:, g, :].unsqueeze(1).to_broadcast([128, 8, 128]), op0=ALU.min, op1=ALU.mult),
              reads=[B("LT", g), T("CBM")], writes=[B("LT", g)])
            psa, pba = pbank(2)
            psy, pby = pbank(2)
            E("pe", lambda e: e.matmul(psa[:, :], lhsT=identB[:], rhs=XSD[:, g * 512:(g + 1) * 512], start=True, stop=False),
              reads=[T("identB"), T("XSD")], writes=[pba])
            for hh in range(8):
                h = g * 8 + hh
                E("pe", lambda e, hh=hh, h=h: e.matmul(psa[:, hh * 64:(hh + 1) * 64], lhsT=LT[:, h, :], rhs=XDT[:, h * 64:(h + 1) * 64], start=False, stop=(hh == 7)),
                  reads=[B("LT", g), T("XDT")], writes=[pba])
            E("pe", lambda e: e.matmul(psy[:, :], lhsT=BCT[:, 2 + g, :], rhs=STB[:, g * 512:(g + 1) * 512], start=True, stop=True),
              reads=[B("BCT", p, 2 + g), T("STB")], writes=[pby])
            E("dve", lambda e: e.tensor_tensor(out=Y32[:, g * 512:(g + 1) * 512].rearrange("p (h q) -> p h q", h=8),
                                               in0=psy[:, :].rearrange("p (h q) -> p h q", h=8),
                                               in1=ecs[:, g * 8:(g + 1) * 8].unsqueeze(2).to_broadcast([128, 8, 64]), op=ALU.mult),
              reads=[pby, b_ecs], writes=[bY])
            E("dve", lambda e: e.tensor_tensor(out=Y32[:, g * 512:(g + 1) * 512], in0=psa[:, :], in1=Y32[:, g * 512:(g + 1) * 512], op=ALU.add),
              reads=[pba, bY], writes=[bY])
        V += [lambda: v_y(0), lambda: v_y(1)]

        def v_state():
            for g in range(2):
                ps, pb_ = pbank(2)
                E("pe", lambda e, g=g, ps=ps: e.matmul(ps[:, :], lhsT=BTM[:, g * 128:(g + 1) * 128], rhs=XDD[:, g * 512:(g + 1) * 512], start=True, stop=True),
                  reads=[B("BTM", p), T("XDD")], writes=[pb_])
                sv = ST32[:, g * 512:(g + 1) * 512]
                E("dve", lambda e, sv=sv, g=g: e.tensor_tensor(out=sv.rearrange("p (h q) -> p h q", h=8), in0=sv.rearrange("p (h q) -> p h q", h=8),
                                                               in1=etot[:, g * 8:(g + 1) * 8].unsqueeze(2).to_broadcast([128, 8, 64]), op=ALU.mult),
                  reads=[T("ST32"), b_etot], writes=[T("ST32")])
                E("dve", lambda e, ps=ps, sv=sv: e.tensor_tensor(out=sv, in0=ps[:, :], in1=sv, op=ALU.add), reads=[pb_, T("ST32")], writes=[T("ST32")])
            E("act", lambda e: e.activation(out=STB[:, :], in_=ST32[:, :], func=AF.Copy), reads=[T("ST32")], writes=[T("STB")])
        V.append(v_state)

        def v_gate():
            E("dve", lambda e: e.tensor_mul(out=Y32[:], in0=Y32[:], in1=SZ[:]), reads=[bY, bSZ], writes=[bY])
            for g in range(2):
                E("act", lambda e, g=g: e.activation(out=XSD[:, g * 512:(g + 1) * 512], in_=Y32[:, g * 512:(g + 1) * 512], func=AF.Square, accum_out=ssq2[:, g:g + 1]),
                  reads=[bY], writes=[T("XSD"), b_ssq2])
            rstd_from_ssq(ssq2, b_ssq2, 512.0, rs2, b_rs2)
            for g in range(2):
                E("act", lambda e, g=g: e.activation(out=XSD[:, g * 512:(g + 1) * 512], in_=Y32[:, g * 512:(g + 1) * 512], func=AF.Copy, scale=rs2[:, g:g + 1]),
                  reads=[bY, b_rs2], writes=[T("XSD")])
            for k in range(8):
                E("pe", lambda e, k=k: e.transpose(out=PSB2[1][:, k * 128:(k + 1) * 128], in_=XSD[:, k * 128:(k + 1) * 128], identity=identB[:]),
                  reads=[T("XSD"), T("identB")], writes=[bPSB2[1]])
            E("act", lambda e: e.activation(out=MIXT2[:, :, :], in_=PSB2[1][:, :].rearrange("p (k t) -> p k t", k=8), func=AF.Copy), reads=[bPSB2[1]], writes=[T("MIXT2")])
            P.dma(lambda e: e.dma_start(out=OUT[:], in_=x_d[c * 128:(c + 1) * 128, :]), f"stg{sl}", writes=[bOUT])
        V.append(v_gate)

        def v_out(half):
            ps, pb_ = pbank(2)
            for k in range(16):
                if k < 8:
                    E("pe", lambda e, k=k: e.matmul(ps[:, :], lhsT=MIXT[:, k, c * 128:(c + 1) * 128], rhs=WOUT[:, k, half * 512:(half + 1) * 512],
                                                    start=(k == 0), stop=False), reads=[T("MIXT"), T("WOUT")], writes=[pb_])
                else:
                    E("pe", lambda e, k=k: e.matmul(ps[:, :], lhsT=MIXT2[:, k - 8, :], rhs=WOUT[:, k, half * 512:(half + 1) * 512],
                                                    start=False, stop=(k == 15)), reads=[T("MIXT2"), T("WOUT")], writes=[pb_])
            E("dve", lambda e: e.tensor_tensor(out=OUT[:, half * 512:(half + 1) * 512], in0=ps[:, :], in1=OUT[:, half * 512:(half + 1) * 512], op=ALU.add),
              reads=[pb_, bOUT], writes=[bOUT])
        TAIL = [lambda: v_out(0), lambda: v_out(1)]

        def v_final():
            E("act", lambda e: e.activation(out=XDT[:], in_=OUT[:], func=AF.Square, accum_out=ssqf), reads=[bOUT], writes=[T("XDT"), b_ssqf])
            rstd_from_ssq(ssqf, b_ssqf, 1024.0, rsf, b_rsf)
            E("dve", lambda e: e.scalar_tensor_tensor(out=OUT[:], in0=OUT[:], scalar=rsf, in1=PBT[:, 48:1072], op0=ALU.mult, op1=ALU.mult),
              reads=[bOUT, b_rsf, T("PBT")], writes=[bOUT])
            P.dma(lambda e: e.dma_start(out=y_d[c * 128:(c + 1) * 128, :], in_=OUT[:]), f"yout{sl}", reads=[bOUT])
        TAIL.append(v_final)
        return V, TAIL

    x_load(0)
    for u in B_S1(0):
        u()
    tail_prev = []
    for c in range(NCH):
        U = B_S1(c + 1) if c + 1 < NCH else []
        head, tail = B_S2(c)
        V = tail_prev + head
        tail_prev = tail
        if U:
            zipper(U, V, 0.15, 1.0)
        else:
            for v in V:
                v()
    for v in tail_prev:
        v()

    replay_all(final=True)
    for g in reversed(stack):
        g.__exit__(None, None, None)
    for g in reversed(pstack):
        g.__exit__(None, None, None)
    for g in reversed(semstack):
        g.__exit__(None, None, None)
    global LASTP
    LASTP = P
    return nc, dbg_d


def pack_params(norm_w, hg_lb_logits, hg_norm_w, m2_conv_w, m2_conv_b, m2_dt_bias, m2_a_log, m2_d_skip, m2_norm_w, final_norm_w):
    pp = np.zeros((128, NPP), np.float32)
    pp[:, 0:8] = norm_w.reshape(8, 128).T
    pp[:, 8:24] = np.concatenate([hg_norm_w.reshape(-1), m2_norm_w.reshape(-1)]).reshape(16, 128).T
    pp[:, 24:32] = hg_lb_logits[0].reshape(8, 128).T
    pp[:, 32:40] = hg_lb_logits[1].reshape(8, 128).T
    cw = m2_conv_w.reshape(4, 12, 128)
    pp[:, 40:88] = cw.transpose(2, 1, 0).reshape(128, 48)
    pp[:, 88:100] = m2_conv_b.reshape(12, 128).T
    pb = np.zeros((1, NPB), np.float32)
    pb[0, 0:16] = m2_dt_bias.reshape(-1)
    pb[0, 16:32] = m2_a_log.reshape(-1)
    pb[0, 32:48] = m2_d_skip.reshape(-1)
    pb[0, 48:1072] = final_norm_w.reshape(-1)
    pb[0, 1072:2608] = m2_conv_b.reshape(-1)
    return pp, pb


_CACHE = {}


def run(x, norm_w, w_in, hg_lb_logits, hg_norm_w, m2_conv_w, m2_conv_b, m2_dt_bias, m2_a_log, m2_d_skip, m2_norm_w, w_out,
        final_norm_w, dbg=()):
    x = np.asarray(x, np.float32)
    Bn, L, _ = x.shape
    nc, dbg_d = build(L, dbg)
    pp, pb = pack_params(np.asarray(norm_w[0]), np.asarray(hg_lb_logits), np.asarray(hg_norm_w[0]), np.asarray(m2_conv_w[0]),
                         np.asarray(m2_conv_b[0]), np.asarray(m2_dt_bias[0]), np.asarray(m2_a_log[0]), np.asarray(m2_d_skip[0]),
                         np.asarray(m2_norm_w[0]), np.asarray(final_norm_w))
    win = np.ascontiguousarray(np.asarray(w_in[0], np.float32))
    wout = np.ascontiguousarray(np.asarray(w_out[0], np.float32))
    in_maps = [{"x": np.ascontiguousarray(x[b]), "w_in": win, "w_out": wout, "pp": pp, "pb": pb} for b in range(Bn)]
    res = run_bass_kernel_spmd(nc, in_maps, core_ids=list(range(Bn)))
    out = np.stack([np.asarray(r["y"]) for r in res.results], axis=0).astype(np.float32)
    if dbg:
        return out, [{k: np.asarray(r["dbg_" + k]) for k in dbg_d} for r in res.results]
    return out


def kernel(x, norm_w, w_in, hg_lb_logits, hg_norm_w, m2_conv_w, m2_conv_b, m2_dt_bias, m2_a_log, m2_d_skip, m2_norm_w, w_out,
           final_norm_w):
    return run(x, norm_w, w_in, hg_lb_logits, hg_norm_w, m2_conv_w, m2_conv_b, m2_dt_bias, m2_a_log, m2_d_skip, m2_norm_w,
               w_out, final_norm_w)
```
